# Optimizing a Trainium2 kernel written in Bass

```python
import math
import jax, jax.numpy as jnp
from jax import lax
import numpy as np

D_MODEL = 1024
BATCH = 8
SEQ = 4096
DEPTH = 2

CHUNK = 64
EPS = 1e-6
D_FF = 4 * D_MODEL

SSD_HEADS = 16
SSD_HEAD_DIM = 64
SSD_INNER = SSD_HEADS * SSD_HEAD_DIM
SSD_GROUPS = 2
SSD_STATE = 128
SSD_CONV = 4
SSD_CONV_DIM = SSD_INNER + 2 * SSD_GROUPS * SSD_STATE
DT_MIN = 0.001
DT_MAX = 0.1

SWA_HEADS = 16
SWA_KV_HEADS = 4
SWA_HEAD_DIM = 64
SWA_WINDOW = 128
SWA_BLOCK = 128
SWA_WINDOW_CHUNKS = SWA_WINDOW // CHUNK

EVEN_IN = SSD_INNER + SSD_CONV_DIM + SSD_HEADS + (SWA_HEADS + 2 * SWA_KV_HEADS) * SWA_HEAD_DIM
EVEN_MIX = SSD_INNER + SWA_HEADS * SWA_HEAD_DIM

DIFF_HEADS = 8
DIFF_HEAD_DIM = 64
DIFF_MIX = 2 * DIFF_HEADS * DIFF_HEAD_DIM
DIFF_IN = 3 * DIFF_MIX
Q_BLOCK = 128

N_EVEN = (DEPTH + 1) // 2
N_ODD = DEPTH // 2

kernel_name = 'hybrid_ssd_swa_diffattn_trunk'


def rms_norm(x, w):
    xf = x.astype(jnp.float32)
    y = xf * lax.rsqrt(jnp.mean(xf * xf, axis=-1, keepdims=True) + EPS)
    return (y * w.astype(jnp.float32)).astype(x.dtype)


def alibi_slopes(n):
    return 2.0 ** (-8.0 * jnp.arange(1, n + 1, dtype=jnp.float32) / n)


def causal_depthwise_conv(x, w, b):
    k = w.shape[0]
    y = lax.conv_general_dilated(x, w[:, None, :].astype(x.dtype), window_strides=(1,),
                                 padding=((k - 1, 0),), dimension_numbers=('NWC', 'WIO', 'NWC'),
                                 feature_group_count=x.shape[-1])
    return y + b


def ssd_scan(x, dt, a, b_mat, c_mat):
    bsz, t, h, p = x.shape
    g, n = b_mat.shape[-2:]
    e = h // g
    nc = t // CHUNK
    f32 = jnp.float32
    xdt = (x.astype(f32) * dt[..., None]).reshape(bsz, nc, CHUNK, g, e, p)
    bm = b_mat.astype(f32).reshape(bsz, nc, CHUNK, g, n)
    cm = c_mat.astype(f32).reshape(bsz, nc, CHUNK, g, n)
    a_dt = (dt * a).reshape(bsz, nc, CHUNK, g, e).transpose(0, 3, 4, 1, 2)
    a_cum = jnp.cumsum(a_dt, axis=-1)
    seg = a_cum[..., :, None] - a_cum[..., None, :]
    causal = jnp.tril(jnp.ones((CHUNK, CHUNK), dtype=bool))
    decay = jnp.exp(jnp.where(causal, seg, -jnp.inf))
    cb = jnp.einsum('bclgn,bcsgn->bgcls', cm, bm)
    y_diag = jnp.einsum('bgecls,bcsgep->bclgep', cb[:, :, None] * decay, xdt)
    decay_to_end = jnp.exp(a_cum[..., -1:] - a_cum).transpose(0, 3, 4, 1, 2)
    chunk_states = jnp.einsum('bclgn,bclgep->cbgepn', bm, xdt * decay_to_end[..., None])
    chunk_decay = jnp.exp(a_cum[..., -1]).transpose(3, 0, 1, 2)

    def step(state, inp):
        dec, new = inp
        return state * dec[..., None, None] + new, state

    init = jnp.zeros((bsz, g, e, p, n), f32)
    _, prev = lax.scan(step, init, (chunk_decay, chunk_states))
    decay_in = jnp.exp(a_cum).transpose(0, 3, 4, 1, 2)
    y_off = jnp.einsum('bclgn,cbgepn->bclgep', cm, prev) * decay_in[..., None]
    return (y_diag + y_off).reshape(bsz, t, h, p)


def swa_attention(q, k, v, q_norm, k_norm, sinks):
    bsz, t = q.shape[:2]
    nb = t // SWA_BLOCK
    r = SWA_HEADS // SWA_KV_HEADS
    f32 = jnp.float32
    q = rms_norm(q, q_norm).reshape(bsz, nb, SWA_BLOCK, SWA_KV_HEADS, r, SWA_HEAD_DIM)
    k = rms_norm(k, k_norm)

    def band(z):
        zb = z.reshape(bsz, nb, SWA_BLOCK, SWA_KV_HEADS, SWA_HEAD_DIM)
        prev = jnp.pad(zb[:, :-1], ((0, 0), (1, 0), (0, 0), (0, 0), (0, 0)))
        return jnp.concatenate([prev, zb], axis=2)

    kb, vb = band(k), band(v)
    s = jnp.einsum('bnqhrd,bnkhd->bhrnqk', q, kb, preferred_element_type=f32) * (SWA_HEAD_DIM ** -0.5)
    qpos = jnp.arange(t).reshape(nb, SWA_BLOCK)
    kpos = qpos[:, :1] - SWA_BLOCK + jnp.arange(2 * SWA_BLOCK)
    dchunk = qpos[:, :, None] // CHUNK - kpos[:, None, :] // CHUNK
    valid = (kpos[:, None, :] >= 0) & (dchunk >= 0) & (dchunk <= SWA_WINDOW_CHUNKS)
    dist = jnp.abs(qpos[:, :, None] - kpos[:, None, :]).astype(f32)
    slopes = alibi_slopes(SWA_HEADS).reshape(1, SWA_KV_HEADS, r, 1, 1, 1)
    s = jnp.where(valid, s - slopes * dist, -jnp.inf)
    sink = sinks.astype(f32).reshape(1, SWA_KV_HEADS, r, 1, 1)
    m = jnp.maximum(jnp.max(s, axis=-1), sink)
    ex = jnp.exp(s - m[..., None])
    prob = ex / (jnp.sum(ex, axis=-1, keepdims=True) + jnp.exp(sink - m)[..., None])
    o = jnp.einsum('bhrnqk,bnkhd->bnqhrd', prob.astype(v.dtype), vb)
    return o.reshape(bsz, t, SWA_HEADS * SWA_HEAD_DIM)


def even_mixer(h, w_in, conv_w, conv_b, dt_bias, a_log, d_skip, ssd_norm_w, q_norm, k_norm, sinks, w_out):
    bsz, t, _ = h.shape
    proj = h @ w_in
    cuts = [SSD_INNER, SSD_INNER + SSD_CONV_DIM, SSD_INNER + SSD_CONV_DIM + SSD_HEADS,
            SSD_INNER + SSD_CONV_DIM + SSD_HEADS + SWA_HEADS * SWA_HEAD_DIM,
            SSD_INNER + SSD_CONV_DIM + SSD_HEADS + (SWA_HEADS + SWA_KV_HEADS) * SWA_HEAD_DIM]
    z, xbc, dt_raw, q, k, v = jnp.split(proj, cuts, axis=-1)
    xbc = jax.nn.silu(causal_depthwise_conv(xbc, conv_w, conv_b))
    xs, bm, cm = jnp.split(xbc, [SSD_INNER, SSD_INNER + SSD_GROUPS * SSD_STATE], axis=-1)
    xs = xs.reshape(bsz, t, SSD_HEADS, SSD_HEAD_DIM)
    dt = jax.nn.softplus(dt_raw.astype(jnp.float32) + dt_bias.astype(jnp.float32))
    a = -jnp.exp(a_log.astype(jnp.float32))
    y = ssd_scan(xs, dt, a, bm.reshape(bsz, t, SSD_GROUPS, SSD_STATE), cm.reshape(bsz, t, SSD_GROUPS, SSD_STATE))
    y = y + xs.astype(jnp.float32) * d_skip.astype(jnp.float32)[:, None]
    y = y.reshape(bsz, t, SSD_INNER) * jax.nn.silu(z.astype(jnp.float32))
    y = rms_norm(y.reshape(bsz, t, SSD_GROUPS, SSD_INNER // SSD_GROUPS), ssd_norm_w.reshape(SSD_GROUPS, -1))
    y_ssd = y.reshape(bsz, t, SSD_INNER).astype(h.dtype)
    y_swa = swa_attention(q.reshape(bsz, t, SWA_HEADS, SWA_HEAD_DIM),
                          k.reshape(bsz, t, SWA_KV_HEADS, SWA_HEAD_DIM),
                          v.reshape(bsz, t, SWA_KV_HEADS, SWA_HEAD_DIM), q_norm, k_norm, sinks).astype(h.dtype)
    return jnp.concatenate([y_ssd, y_swa], axis=-1) @ w_out


def diff_attention(h, w_in, q_norm, k_norm, lam_q1, lam_k1, lam_q2, lam_k2, sub_norm, w_out, layer_idx):
    bsz, t, _ = h.shape
    f32 = jnp.float32
    q, k, v = jnp.split(h @ w_in, 3, axis=-1)
    q = rms_norm(q.reshape(bsz, t, DIFF_HEADS, 2, DIFF_HEAD_DIM), q_norm)
    k = rms_norm(k.reshape(bsz, t, DIFF_HEADS, 2, DIFF_HEAD_DIM), k_norm)
    v = v.reshape(bsz, t, DIFF_HEADS, 2 * DIFF_HEAD_DIM)
    lam_init = 0.8 - 0.6 * math.exp(-0.3 * layer_idx)
    lam = (jnp.exp(jnp.sum(lam_q1.astype(f32) * lam_k1.astype(f32)))
           - jnp.exp(jnp.sum(lam_q2.astype(f32) * lam_k2.astype(f32))) + lam_init)
    slopes = alibi_slopes(DIFF_HEADS)[None, :, None, None, None]
    scale = DIFF_HEAD_DIM ** -0.5
    outs = []
    for i in range(t // Q_BLOCK):
        q0 = i * Q_BLOCK
        kend = q0 + Q_BLOCK
        s = jnp.einsum('bqhmd,bkhmd->bhmqk', q[:, q0:kend], k[:, :kend], preferred_element_type=f32) * scale
        qpos = q0 + jnp.arange(Q_BLOCK)
        kpos = jnp.arange(kend)
        dist = jnp.abs(qpos[:, None] - kpos[None, :]).astype(f32)
        mask = (kpos[None, :] // CHUNK) <= (qpos[:, None] // CHUNK)
        s = jnp.where(mask, s - slopes * dist, -jnp.inf)
        prob = jax.nn.softmax(s, axis=-1)
        attn = prob[:, :, 0] - lam * prob[:, :, 1]
        outs.append(jnp.einsum('bhqk,bkhd->bqhd', attn.astype(v.dtype), v[:, :kend]))
    o = jnp.concatenate(outs, axis=1)
    o = (rms_norm(o, sub_norm).astype(f32) * (1.0 - lam_init)).astype(h.dtype)
    return o.reshape(bsz, t, DIFF_MIX) @ w_out


def setup_inputs(seed: int = 0) -> dict:
    key = jax.random.key(seed)
    ks = iter(jax.random.split(key, 32))

    def nrm(shape, scale):
        return scale * jax.random.normal(next(ks), shape, jnp.float32)

    x = nrm((BATCH, SEQ, D_MODEL), 1.0)
    ev_norm_w = 1.0 + nrm((N_EVEN, D_MODEL), 0.02)
    ev_w_in = nrm((N_EVEN, D_MODEL, EVEN_IN), D_MODEL ** -0.5)
    ev_conv_w = nrm((N_EVEN, SSD_CONV, SSD_CONV_DIM), SSD_CONV ** -0.5)
    ev_conv_b = nrm((N_EVEN, SSD_CONV_DIM), 0.02)
    u = jax.random.uniform(next(ks), (N_EVEN, SSD_HEADS), jnp.float32)
    dt0 = jnp.maximum(jnp.exp(u * (math.log(DT_MAX) - math.log(DT_MIN)) + math.log(DT_MIN)), 1e-4)
    ev_dt_bias = dt0 + jnp.log(-jnp.expm1(-dt0))
    ev_a_log = jnp.log(jax.random.uniform(next(ks), (N_EVEN, SSD_HEADS), jnp.float32, 1.0, 16.0))
    ev_d_skip = 1.0 + nrm((N_EVEN, SSD_HEADS), 0.02)
    ev_ssd_norm_w = 1.0 + nrm((N_EVEN, SSD_INNER), 0.02)
    ev_q_norm = 1.0 + nrm((N_EVEN, SWA_HEAD_DIM), 0.02)
    ev_k_norm = 1.0 + nrm((N_EVEN, SWA_HEAD_DIM), 0.02)
    ev_sinks = nrm((N_EVEN, SWA_HEADS), 0.5)
    ev_w_out = nrm((N_EVEN, EVEN_MIX, D_MODEL), EVEN_MIX ** -0.5)
    od_norm_w = 1.0 + nrm((N_ODD, D_MODEL), 0.02)
    od_w_in = nrm((N_ODD, D_MODEL, DIFF_IN), D_MODEL ** -0.5)
    od_q_norm = 1.0 + nrm((N_ODD, DIFF_HEAD_DIM), 0.02)
    od_k_norm = 1.0 + nrm((N_ODD, DIFF_HEAD_DIM), 0.02)
    od_lam_q1 = nrm((N_ODD, DIFF_HEAD_DIM), 0.1)
    od_lam_k1 = nrm((N_ODD, DIFF_HEAD_DIM), 0.1)
    od_lam_q2 = nrm((N_ODD, DIFF_HEAD_DIM), 0.1)
    od_lam_k2 = nrm((N_ODD, DIFF_HEAD_DIM), 0.1)
    od_sub_norm = 1.0 + nrm((N_ODD, 2 * DIFF_HEAD_DIM), 0.02)
    od_w_out = nrm((N_ODD, DIFF_MIX, D_MODEL), DIFF_MIX ** -0.5)
    mlp_norm_w = 1.0 + nrm((DEPTH, D_MODEL), 0.02)
    mlp_w1 = nrm((DEPTH, D_MODEL, D_FF), D_MODEL ** -0.5)
    mlp_w2 = nrm((DEPTH, D_FF, D_MODEL), D_FF ** -0.5)
    return {'x': x, 'ev_norm_w': ev_norm_w, 'ev_w_in': ev_w_in, 'ev_conv_w': ev_conv_w, 'ev_conv_b': ev_conv_b,
            'ev_dt_bias': ev_dt_bias, 'ev_a_log': ev_a_log, 'ev_d_skip': ev_d_skip, 'ev_ssd_norm_w': ev_ssd_norm_w,
            'ev_q_norm': ev_q_norm, 'ev_k_norm': ev_k_norm, 'ev_sinks': ev_sinks, 'ev_w_out': ev_w_out,
            'od_norm_w': od_norm_w, 'od_w_in': od_w_in, 'od_q_norm': od_q_norm, 'od_k_norm': od_k_norm,
            'od_lam_q1': od_lam_q1, 'od_lam_k1': od_lam_k1, 'od_lam_q2': od_lam_q2, 'od_lam_k2': od_lam_k2,
            'od_sub_norm': od_sub_norm, 'od_w_out': od_w_out,
            'mlp_norm_w': mlp_norm_w, 'mlp_w1': mlp_w1, 'mlp_w2': mlp_w2}


def reference(x, ev_norm_w, ev_w_in, ev_conv_w, ev_conv_b, ev_dt_bias, ev_a_log, ev_d_skip, ev_ssd_norm_w,
              ev_q_norm, ev_k_norm, ev_sinks, ev_w_out,
              od_norm_w, od_w_in, od_q_norm, od_k_norm, od_lam_q1, od_lam_k1, od_lam_q2, od_lam_k2,
              od_sub_norm, od_w_out, mlp_norm_w, mlp_w1, mlp_w2):
    for layer in range(DEPTH):
        j = layer // 2
        if layer % 2 == 0:
            mix = even_mixer(rms_norm(x, ev_norm_w[j]), ev_w_in[j], ev_conv_w[j], ev_conv_b[j], ev_dt_bias[j],
                             ev_a_log[j], ev_d_skip[j], ev_ssd_norm_w[j], ev_q_norm[j], ev_k_norm[j],
                             ev_sinks[j], ev_w_out[j])
        else:
            mix = diff_attention(rms_norm(x, od_norm_w[j]), od_w_in[j], od_q_norm[j], od_k_norm[j],
                                 od_lam_q1[j], od_lam_k1[j], od_lam_q2[j], od_lam_k2[j],
                                 od_sub_norm[j], od_w_out[j], layer)
        x = x + mix.astype(x.dtype)
        hid = rms_norm(x, mlp_norm_w[layer]) @ mlp_w1[layer]
        x = x + (jnp.square(jax.nn.relu(hid)) @ mlp_w2[layer]).astype(x.dtype)
    return x
```

```python
import numpy as np
import concourse.bass as bass
import concourse.mybir as mybir

F32 = mybir.dt.float32
BF16 = mybir.dt.bfloat16
AF = mybir.ActivationFunctionType
ALU = mybir.AluOpType
AX = mybir.AxisListType

SEM_CH = 2000
N_DMA_SEMS = {"sp": 12, "pool": 4, "act": 2}


_PSUM_NAMES = {"F", "pY", "pH", "pT", "pQ", "pS", "pV", "pO", "pD"}


def _is_psum_key(k):
    if isinstance(k, tuple):
        return k[0] in _PSUM_NAMES
    return k in _PSUM_NAMES


class _Op:
    __slots__ = ("eng", "fn", "reads", "writes", "dma", "idx", "seq", "waits", "signal",
                 "sig", "clock", "tab")

    def __init__(self, eng, fn, reads, writes, dma, idx):
        self.eng, self.fn, self.reads, self.writes, self.dma, self.idx = eng, fn, reads, writes, dma, idx
        self.waits = []
        self.signal = False
        self.sig = None
        self.tab = None


SCHED = [True]
SCHED_P = [450.0, 60.0, 400]


class _Rec:
    def __init__(self):
        self.calls = []

    def __getattr__(self, name):
        def f(*a, **k):
            self.calls.append((name, a, k))
            return self
        return f


class SemPool:
    _inst = {}

    @classmethod
    def get(cls, nc):
        if id(nc) not in cls._inst:
            cls._inst[id(nc)] = cls(nc)
        return cls._inst[id(nc)]

    def __init__(self, nc):
        import contextlib
        self.nc = nc
        self.st = contextlib.ExitStack()
        self.esems = {e: [] for e in Prog.ENGS}
        self.sigc = {e: 0 for e in Prog.ENGS}
        self.dsems = {q: [self.st.enter_context(nc.semaphore(f"d_{q}_{j}")) for j in range(n)]
                      for q, n in N_DMA_SEMS.items()}
        self.dcount = {q: [0] * n for q, n in N_DMA_SEMS.items()}
        self.drr = {q: 0 for q in N_DMA_SEMS}

    def eng_sem(self, e, j):
        while len(self.esems[e]) <= j:
            self.esems[e].append(self.st.enter_context(self.nc.semaphore(f"s_{e}_{len(self.esems[e])}")))
        return self.esems[e][j]


class Prog:
    ENGS = ("pe", "act", "dve", "pool", "sp")
    PH = [0]

    def __init__(self, nc):
        Prog.PH[0] += 1
        self.nc = nc
        self.ops = []
        self.last_w = {}
        self.readers = {}

    def add(self, eng, fn, reads=(), writes=(), dma=False):
        idx = len(self.ops)
        op = _Op(eng, fn, tuple(reads), tuple(writes), dma, idx)
        raw, other = set(), set()
        excl = [k for k in op.reads if _is_psum_key(k)]
        for k in op.reads:
            w = self.last_w.get(k)
            if w is not None:
                raw.add(w)
        for k in excl:
            for r in self.readers.get(k, ()):
                other.add(r)
        for k in op.writes:
            w = self.last_w.get(k)
            if w is not None:
                other.add(w)
            for r in self.readers.get(k, ()):
                other.add(r)
        for k in op.reads:
            self.readers.setdefault(k, []).append(idx)
        for k in op.writes:
            self.last_w[k] = idx
            self.readers[k] = []
        for k in excl:
            if k not in op.writes:
                self.last_w[k] = idx
                self.readers[k] = []
        other -= raw
        other.discard(idx)
        raw.discard(idx)
        op.waits = (raw, other)
        self.ops.append(op)
        return op

    def pe(self, fn, reads=(), writes=()):
        return self.add("pe", fn, reads, writes)

    def act(self, fn, reads=(), writes=()):
        return self.add("act", fn, reads, writes)

    def dve(self, fn, reads=(), writes=()):
        return self.add("dve", fn, reads, writes)

    def pool(self, fn, reads=(), writes=()):
        return self.add("pool", fn, reads, writes)

    def dma(self, q, out, in_, reads=(), writes=(), **kw):
        return self.add(q, lambda e: e.dma_start(out=out, in_=in_, **kw), reads, writes, dma=True)

    def _est(self, op):
        rec = _Rec()
        try:
            op.fn(rec)
        except Exception:
            pass
        if not rec.calls:
            return 200.0, 200.0
        name, a, k = rec.calls[0]

        def fsz(ap):
            try:
                shp = ap.shape
                n = 1
                for d in shp[1:]:
                    n *= int(d)
                return n
            except Exception:
                return 256

        if op.dma:
            out = k.get("out", a[0] if a else None)
            n = fsz(out) * 128 * 4
            issue = 1500.0 if op.eng == "pool" else 120.0
            return issue, issue + 2500.0 + n / 120.0
        if op.eng == "pe":
            if name == "transpose":
                return 70.0, 300.0
            rhs = k.get("rhs", a[2] if len(a) > 2 else None)
            n = max(fsz(rhs), 64)
            f = 4.0 if str(getattr(rhs, "dtype", "")).endswith("float32") else 1.0
            d = f * n / 2.3 + 10
            return d, d + 150.0
        out = k.get("out", a[0] if a else None)
        n = fsz(out)
        if op.eng == "act":
            fn_ = str(k.get("func", ""))
            op.tab = "S" if "Silu" in fn_ else ("E" if ("Exp" in fn_ or "Ln" in fn_) else None)
            d = 210.0 + n / 1.25 + (100.0 if k.get("accum_out") is not None else 0.0)
        elif op.eng == "dve":
            d = 80.0 + n / 0.96
        else:
            d = 150.0 + n * 1.2
        return d, d + 100.0

    def schedule(self):
        import heapq
        ops = self.ops
        n = len(ops)
        preds = [set(op.waits[0]) | set(op.waits[1]) for op in ops]
        succs = [[] for _ in range(n)]
        indeg = [0] * n
        for i, ps in enumerate(preds):
            indeg[i] = len(ps)
            for p in ps:
                succs[p].append(i)
        est = [self._est(op) for op in ops]
        cp = [0.0] * n
        for i in range(n - 1, -1, -1):
            m = 0.0
            for j in succs[i]:
                if cp[j] > m:
                    m = cp[j]
            cp[i] = est[i][1] + m
        eng_t = {e: 0.0 for e in self.ENGS}
        fin = [0.0] * n
        ready_at = [0.0] * n
        ready = [i for i in range(n) if indeg[i] == 0]
        order = []
        LAT_X, LAT_S = SCHED_P[0], SCHED_P[1]
        cur_tab = [None]
        TAB_SWITCH = 1300.0
        WINDOW = SCHED_P[2]
        while ready:
            best, bkey = None, None
            lo = min(ready)
            for i in ready:
                if i > lo + WINDOW:
                    continue
                st = max(eng_t[ops[i].eng], ready_at[i])
                tb = ops[i].tab
                if tb is not None and tb != cur_tab[0]:
                    st += TAB_SWITCH
                key = (st, -cp[i], i)
                if bkey is None or key < bkey:
                    best, bkey = i, key
            i = best
            ready.remove(i)
            st = bkey[0]
            if ops[i].tab is not None:
                cur_tab[0] = ops[i].tab
            eng_t[ops[i].eng] = st + est[i][0]
            fin[i] = st + est[i][1]
            order.append(i)
            for j in succs[i]:
                lat = LAT_S if ops[j].eng == ops[i].eng else LAT_X
                if fin[i] + lat > ready_at[j]:
                    ready_at[j] = fin[i] + lat
                indeg[j] -= 1
                if indeg[j] == 0:
                    ready.append(j)
        assert len(order) == n
        old = [ops[i] for i in order]
        self.ops = []
        self.last_w = {}
        self.readers = {}
        for o in old:
            self.add(o.eng, o.fn, o.reads, o.writes, o.dma)
        self.sim_time = max(fin) if fin else 0.0

    def finalize(self):
        ops = self.ops
        E = self.ENGS
        seqc = {e: 0 for e in E}
        clock = {e: {x: 0 for x in E} for e in E}
        for op in ops:
            seqc[op.eng] += 1
            op.seq = seqc[op.eng]
            raw, other = op.waits
            ck = clock[op.eng]
            need = []
            for i in sorted(raw | other, key=lambda i: -ops[i].seq):
                p = ops[i]
                if p.dma:
                    need.append(i)
                    pc = p.clock
                    for e2 in E:
                        if pc[e2] > ck[e2]:
                            ck[e2] = pc[e2]
                    continue
                if p.eng == op.eng and not op.dma:
                    if p.eng == "pe" or i not in raw:
                        continue
                if ck[p.eng] >= p.seq:
                    continue
                need.append(i)
                pc = p.clock
                for e2 in E:
                    if pc[e2] > ck[e2]:
                        ck[e2] = pc[e2]
            for i in need:
                ops[i].signal = True
            op.waits = need
            snap = dict(ck)
            if not op.dma:
                snap[op.eng] = max(snap[op.eng], 0)
                snap = dict(snap)
                snap[op.eng] = op.seq
            op.clock = snap

    def emit(self, sched=True):
        nc = self.nc
        if sched and SCHED[0]:
            self.schedule()
        self.finalize()
        ops = self.ops
        pool = SemPool.get(nc)
        for op in ops:
            if op.signal and not op.dma:
                c = pool.sigc[op.eng]
                pool.sigc[op.eng] += 1
                op.sig = (op.eng, c // SEM_CH, c % SEM_CH + 1)
                pool.eng_sem(op.eng, c // SEM_CH)
        import contextlib
        with contextlib.ExitStack() as st:
            esems = pool.esems
            dsems = pool.dsems
            dcount = pool.dcount
            drr = pool.drr
            for op in ops:
                if op.dma:
                    q = op.eng
                    j = drr[q]
                    drr[q] = (j + 1) % N_DMA_SEMS[q]
                    prev = dcount[q][j]
                    dcount[q][j] += 16
                    op.sig = ("dma", q, j, dcount[q][j], prev)
            block = st.enter_context(nc.Block())
            per_eng = {e: [op for op in ops if op.eng == e] for e in self.ENGS}
            self.stats = {e: [len(per_eng[e]), 0] for e in self.ENGS}

            def run(eng_name, eng):
                waited = {}
                nw = 0
                for op in per_eng[eng_name]:
                    if op.dma:
                        _, q, j, val, prev = op.sig
                        if prev > 0 and waited.get(("d", q, j), 0) < prev:
                            eng.wait_ge(dsems[q][j], prev)
                            waited[("d", q, j)] = prev
                            nw += 1
                    for i in op.waits:
                        p = ops[i]
                        if p.dma:
                            _, q, j, val, _ = p.sig
                            key = ("d", q, j)
                            sem = dsems[q][j]
                        else:
                            e2, sj, val = p.sig
                            key = ("e", e2, sj)
                            sem = esems[e2][sj]
                        if waited.get(key, 0) >= val:
                            continue
                        eng.wait_ge(sem, val)
                        waited[key] = val
                        nw += 1
                    ins = op.fn(eng)
                    if op.dma:
                        _, q, j, val, _ = op.sig
                        ins.then_inc(dsems[q][j], 16)
                    elif op.signal:
                        e2, sj, val = op.sig
                        ins.then_inc(esems[e2][sj], 1)
                if eng_name in N_DMA_SEMS:
                    for j, c in enumerate(dcount[eng_name]):
                        if c > 0:
                            eng.wait_ge(dsems[eng_name][j], c)
                self.stats[eng_name][1] = nw

            @block.tensor
            def _(e):
                run("pe", e)

            @block.scalar
            def _(e):
                run("act", e)

            @block.vector
            def _(e):
                run("dve", e)

            @block.gpsimd
            def _(e):
                run("pool", e)

            @block.sync
            def _(e):
                run("sp", e)


import contextlib
import ml_dtypes
from concourse.bass_utils import run_bass_kernel_spmd

T = 4096
DM = 1024
DFF = 4096
NB = T // 128
EPS = 1e-6
NCORES = 8


class Ctx:
    CNT = [0]

    def __init__(self, nc):
        self.nc = nc
        self.st = contextlib.ExitStack()

    def sb(self, shape, dt, name=None):
        Ctx.CNT[0] += 1
        return self.st.enter_context(self.nc.sbuf_tensor(name or f"sb{Ctx.CNT[0]}", list(shape), dt))

    def ps(self, shape, dt, name=None):
        Ctx.CNT[0] += 1
        return self.st.enter_context(self.nc.psum_tensor(name or f"ps{Ctx.CNT[0]}", list(shape), dt))

    def close(self):
        self.st.close()


def rmsnorm_block(P, x_ap, xkey, nw, ident, hb, hbkey, pT, pTkey, hT_dst, hTkey, tmp, cp_eng):
    jk, ssq, std = tmp["jk"], tmp["ssq"], tmp["std"]
    k = tmp["key"]
    P.act(lambda e: e.activation(out=jk[:], in_=x_ap, func=AF.Square, accum_out=ssq[:]),
          reads=[xkey], writes=[("jk", k), ("ssq", k)])
    P.act(lambda e: e.activation(out=std[:], in_=ssq[:], func=AF.Ln, scale=1.0 / DM, bias=EPS),
          reads=[("ssq", k)], writes=[("std", k)])
    P.act(lambda e: e.activation(out=std[:], in_=std[:], func=AF.Exp, scale=-0.5), reads=[("std", k)], writes=[("std", k)])
    P.dve(lambda e: e.scalar_tensor_tensor(out=hb[:], in0=x_ap, scalar=std[:, 0:1], in1=nw[:],
                                           op0=ALU.mult, op1=ALU.mult),
          reads=[xkey, ("std", k), "nw"], writes=[hbkey])
    for c in range(8):
        P.pe(lambda e, c=c: e.transpose(pT[:, c * 128:(c + 1) * 128], hb[:, c * 128:(c + 1) * 128], ident[:]),
             reads=[hbkey, "ident"], writes=[pTkey])
    src = pT[:].rearrange("p (c t) -> p c t", c=8)
    if cp_eng == "act":
        P.act(lambda e: e.copy(hT_dst, src), reads=[pTkey], writes=[hTkey])
    else:
        P.dve(lambda e: e.tensor_copy(hT_dst, src), reads=[pTkey], writes=[hTkey])


def load_consts(P, C, dram):
    identf = C.sb([128, 128], F32)
    ident = C.sb([128, 128], BF16)
    P.dma("sp", identf[:], dram["c_ident"].ap(), writes=["identf"])
    P.dve(lambda e: e.tensor_copy(ident[:], identf[:]), reads=["identf"], writes=["ident"])
    return ident


def phase_mlp(nc, dram, src, dst, layer, pre=None):
    C = Ctx(nc)
    P = Prog(nc)
    TT = 256
    NT = T // TT
    w1 = dram["mlp_w1"].ap()[layer]
    w2 = dram["mlp_w2"].ap()[layer]
    W1s = C.sb([128, 8, DFF], BF16)
    W2s = C.sb([128, 32, DM], BF16)
    nw = C.sb([128, DM], F32)
    ident = load_consts(P, C, dram)
    P.dma("sp", nw[:], dram["mlp_norm_w"].ap()[layer].partition_broadcast(128), writes=["nw"])
    if pre is not None:
        aoT, wo_ap = pre
        Wo = C.sb([128, 8, DM], BF16)
        at = [C.sb([128, 8, TT], BF16) for _ in range(2)]
        for i in range(2):
            P.dma("pool", Wo[:, :, i * 512:(i + 1) * 512],
                  wo_ap.rearrange("(c p) n -> p c n", p=128)[:, :, i * 512:(i + 1) * 512], writes=[("Wo", i)])
    for i in range(8):
        P.dma("pool", W1s[:, :, i * 512:(i + 1) * 512],
              w1.rearrange("(c p) n -> p c n", p=128)[:, :, i * 512:(i + 1) * 512], writes=[("W1", i)])
    for i in range(8):
        P.dma("pool", W2s[:, i * 4:(i + 1) * 4, :],
              w2.rearrange("(c p) n -> p c n", p=128)[:, i * 4:(i + 1) * 4, :], writes=[("W2", i)])
    xt = [C.sb([128, 2, DM], F32) for _ in range(2)]
    hb = [C.sb([128, DM], BF16) for _ in range(2)]
    hT = [C.sb([128, 8, TT], BF16) for _ in range(2)]
    rl = [C.sb([128, TT], F32) for _ in range(2)]
    hid = C.sb([128, 32, TT], BF16)
    tmps = [dict(jk=C.sb([128, DM], BF16), ssq=C.sb([128, 1], F32), std=C.sb([128, 1], F32), key=i) for i in range(2)]
    pT = [C.ps([128, DM], BF16) for _ in range(2)]
    pH = [C.ps([128, 512], F32) for _ in range(2)]
    pY = [C.ps([128, 512], F32) for _ in range(4)]
    srca = src.ap()
    dsta = dst.ap()

    def load(t):
        s = t % 2
        P.dma("sp", xt[s][:], srca[t * TT:(t + 1) * TT, :].rearrange("(b p) d -> p b d", p=128),
              writes=[("xt", s, 0), ("xt", s, 1)])
        if pre is not None:
            P.dma("sp", at[s][:], aoT.ap()[:, t * TT:(t + 1) * TT].rearrange("(c p) t -> p c t", p=128),
                  writes=[("at", s)])

    def norm(t):
        s = t % 2
        for b in range(2):
            if pre is not None:
                for half in range(2):
                    bank = pY[b * 2 + half]
                    for c in range(8):
                        P.pe(lambda e, c=c, bank=bank, b=b, half=half: e.matmul(
                            bank[:], at[s][:, c, b * 128:(b + 1) * 128], Wo[:, c, half * 512:(half + 1) * 512],
                            start=(c == 0), stop=(c == 7)),
                            reads=[("at", s), ("Wo", half)], writes=[("pY", b * 2 + half)])
                    xs = xt[s][:, b, half * 512:(half + 1) * 512]
                    P.dve(lambda e, xs=xs, bank=bank: e.tensor_tensor(out=xs, in0=bank[:], in1=xs, op=ALU.add),
                          reads=[("pY", b * 2 + half), ("xt", s, b)], writes=[("xt", s, b)])
            rmsnorm_block(P, xt[s][:, b, :], ("xt", s, b), nw, ident, hb[b], ("hb", b), pT[b], ("pT", b),
                          hT[s][:, :, b * 128:(b + 1) * 128], ("hT", s, b), tmps[b], "act" if b == 0 else "dve")

    def stage1(t):
        s = t % 2
        for f in range(32):
            bank = pH[f % 2]
            for c in range(8):
                P.pe(lambda e, c=c, f=f, bank=bank: e.matmul(
                    bank[:, 0:TT], W1s[:, c, f * 128:(f + 1) * 128], hT[s][:, c, :],
                    start=(c == 0), stop=(c == 7)),
                    reads=[("W1", f // 4), ("hT", s, 0), ("hT", s, 1)], writes=[("pH", f % 2)])
            r = rl[f % 2]
            P.act(lambda e, r=r, bank=bank: e.activation(out=r[:], in_=bank[:, 0:TT], func=AF.Relu),
                  reads=[("pH", f % 2)], writes=[("rl", f % 2)])
            P.pool(lambda e, r=r, f=f: e.tensor_tensor(out=hid[:, f, :], in0=r[:], in1=r[:], op=ALU.mult),
                   reads=[("rl", f % 2)], writes=[("hid", f)])

    def stage2(t):
        s = t % 2
        for b in range(2):
            for half in range(2):
                bank = pY[b * 2 + half]
                for f in range(32):
                    P.pe(lambda e, f=f, bank=bank, b=b, half=half: e.matmul(
                        bank[:], hid[:, f, b * 128:(b + 1) * 128], W2s[:, f, half * 512:(half + 1) * 512],
                        start=(f == 0), stop=(f == 31)),
                        reads=[("hid", f), ("W2", f // 4)], writes=[("pY", b * 2 + half)])
                xs = xt[s][:, b, half * 512:(half + 1) * 512]
                P.dve(lambda e, xs=xs, bank=bank: e.tensor_tensor(out=xs, in0=bank[:], in1=xs, op=ALU.add),
                      reads=[("pY", b * 2 + half), ("xt", s, b)], writes=[("xt", s, b)])
        P.dma("sp", dsta[t * TT:(t + 1) * TT, :].rearrange("(b p) d -> p b d", p=128), xt[s][:],
              reads=[("xt", s, 0), ("xt", s, 1)], writes=[("dst", t)])

    load(0)
    load(1)
    norm(0)
    for t in range(NT):
        stage1(t)
        if t + 1 < NT:
            norm(t + 1)
        stage2(t)
        if t + 2 < NT:
            load(t + 2)
    P.emit()
    C.close()
    return P.stats


def host_consts():
    c = {}
    c["c_ident"] = np.eye(128, dtype=np.float32)
    return c


IN_SHAPES = {
    "x": [T, DM],
    "ev_norm_w": [1, DM], "ev_w_in": [1, DM, 4112], "ev_conv_wT": [128, 12, 4], "ev_conv_bT": [128, 12],
    "ev_dt_bias": [1, 16], "ev_a_log": [1, 16], "ev_d_skip": [1, 16], "ev_ssd_norm_w": [1, 1024],
    "ev_q_norm": [1, 64], "ev_k_norm": [1, 64], "ev_sinks": [1, 16], "ev_w_out": [1, 2048, DM],
    "od_norm_w": [1, DM], "od_w_in": [1, DM, 3072], "od_q_norm": [1, 64], "od_k_norm": [1, 64],
    "od_lam_q1": [1, 64], "od_lam_k1": [1, 64], "od_lam_q2": [1, 64], "od_lam_k2": [1, 64],
    "od_sub_norm": [1, 128], "od_w_out": [1, DM, DM],
    "mlp_norm_w": [2, DM], "mlp_w1": [2, DM, DFF], "mlp_w2": [2, DFF, DM],
}


def declare(nc, consts, only=None):
    dram = {}
    for k, shp in IN_SHAPES.items():
        if only is not None and k not in only:
            continue
        dram[k] = nc.dram_tensor(k, shp, F32, kind="ExternalInput")
    for k, v in consts.items():
        dt = BF16 if v.dtype == ml_dtypes.bfloat16 else F32
        dram[k] = nc.dram_tensor(k, list(v.shape), dt, kind="ExternalInput")
    return dram


def load_col(P, C, row_ap, n, reps, key, scale=None):
    col = C.sb([n * reps, 1], F32)
    for r in range(reps):
        P.dma("sp", col[r * n:(r + 1) * n, :], row_ap.rearrange("(d o) -> d o", o=1), writes=[(key, "raw", r)])
    if scale is not None:
        col2 = C.sb([n * reps, 1], F32)
        P.dve(lambda e: e.tensor_scalar(col2[:], col[:], float(scale), None, op0=ALU.mult),
              reads=[(key, "raw", r) for r in range(reps)], writes=[key])
        return col2
    P.dve(lambda e: e.tensor_copy(col[:], col[:]), reads=[(key, "raw", r) for r in range(reps)], writes=[key])
    return col


def phase_qkv1(nc, dram, src, qTs, kTs, vs):
    C = Ctx(nc)
    P = Prog(nc)
    TT = 512
    NT = T // TT
    win = dram["od_w_in"].ap()[0]
    Win = C.sb([128, 8, 3072], BF16)
    nw = C.sb([128, DM], F32)
    ident = load_consts(P, C, dram)
    onesbd = C.sb([128, 128], BF16)
    P.dma("sp", onesbd[:], dram["c_onesbd"].ap(), writes=["onesbd"])
    P.dma("sp", nw[:], dram["od_norm_w"].ap()[0].partition_broadcast(128), writes=["nw"])
    wq = load_col(P, C, dram["od_q_norm"].ap()[0], 64, 2, "wq", scale=0.125)
    wk = load_col(P, C, dram["od_k_norm"].ap()[0], 64, 2, "wk")
    for i in range(6):
        P.dma("pool", Win[:, :, i * 512:(i + 1) * 512],
              win.rearrange("(c p) n -> p c n", p=128)[:, :, i * 512:(i + 1) * 512], writes=[("Win", i)])
    xt = [C.sb([128, 4, DM], F32) for _ in range(2)]
    hb = [C.sb([128, DM], BF16) for _ in range(2)]
    hT = [C.sb([128, 8, TT], BF16) for _ in range(2)]
    tmps = [dict(jk=C.sb([128, DM], BF16), ssq=C.sb([128, 1], F32), std=C.sb([128, 1], F32), key=i) for i in range(2)]
    sq = [C.sb([128, TT], BF16) for _ in range(2)]
    sd = [C.sb([128, TT], F32) for _ in range(2)]
    qo = [C.sb([128, TT], BF16) for _ in range(4)]
    qs = [C.sb([128, TT], F32) for _ in range(3)]
    vo = [C.sb([128, DM], BF16) for _ in range(2)]
    pT = [C.ps([128, DM], BF16) for _ in range(2)]
    pQ = [C.ps([128, 512], F32) for _ in range(2)]
    pS = [C.ps([128, 512], F32) for _ in range(2)]
    pV = [C.ps([128, 512], F32) for _ in range(2)]
    srca = src.ap()

    def load(t):
        s = t % 2
        P.dma("sp", xt[s][:], srca[t * TT:(t + 1) * TT, :].rearrange("(b p) d -> p b d", p=128),
              writes=[("xt", s, b) for b in range(4)])

    def norm(t):
        s = t % 2
        for b in range(4):
            rmsnorm_block(P, xt[s][:, b, :], ("xt", s, b), nw, ident, hb[b % 2], ("hb", b % 2), pT[b % 2],
                          ("pT", b % 2), hT[s][:, :, b * 128:(b + 1) * 128], ("hT", s, b), tmps[b % 2],
                          "act" if b % 2 == 0 else "dve")

    cnt = [0]

    def proj(t):
        s = t % 2
        hkeys = [("hT", s, b) for b in range(4)]
        for ch in range(16):
            i = cnt[0] % 2
            o = cnt[0] % 4
            cnt[0] += 1
            for c in range(8):
                P.pe(lambda e, c=c, ch=ch, i=i: e.matmul(pQ[i][:], Win[:, c, ch * 128:(ch + 1) * 128], hT[s][:, c, :],
                                                        start=(c == 0), stop=(c == 7)),
                     reads=[("Win", ch // 4)] + hkeys, writes=[("pQ", i)])
            k3 = (cnt[0] - 1) % 3
            P.dve(lambda e, i=i, k3=k3: e.tensor_copy(qs[k3][:], pQ[i][:]), reads=[("pQ", i)], writes=[("qs", k3)])
            P.act(lambda e, i=i, k3=k3: e.activation(out=sq[i][:], in_=qs[k3][:], func=AF.Square),
                  reads=[("qs", k3)], writes=[("sq", i)])
            P.pe(lambda e, i=i: e.matmul(pS[i][:], onesbd[:], sq[i][:], start=True, stop=True),
                 reads=["onesbd", ("sq", i)], writes=[("pS", i)])
            P.act(lambda e, i=i: e.activation(out=sd[i][:], in_=pS[i][:], func=AF.Ln, scale=1.0 / 64, bias=EPS),
                  reads=[("pS", i)], writes=[("sd", i)])
            P.act(lambda e, i=i: e.activation(out=sd[i][:], in_=sd[i][:], func=AF.Exp, scale=-0.5),
                  reads=[("sd", i)], writes=[("sd", i)])
            wcol = wq if ch < 8 else wk
            P.dve(lambda e, i=i, o=o, wcol=wcol, k3=k3: e.scalar_tensor_tensor(
                out=qo[o][:], in0=qs[k3][:], scalar=wcol[:, 0:1], in1=sd[i][:], op0=ALU.mult, op1=ALU.mult),
                reads=[("qs", k3), ("sd", i), "wq", "wk"], writes=[("qo", o)])
            dstT = qTs if ch < 8 else kTs
            P.dma("sp", dstT.ap()[ch % 8, :, t * TT:(t + 1) * TT], qo[o][:], reads=[("qo", o)],
                  writes=[("qk", ch, t)])
        for b in range(4):
            for half in range(2):
                for c in range(8):
                    P.pe(lambda e, c=c, b=b, half=half: e.matmul(
                        pV[half][:], hT[s][:, c, b * 128:(b + 1) * 128],
                        Win[:, c, 2048 + half * 512:2048 + (half + 1) * 512], start=(c == 0), stop=(c == 7)),
                        reads=[("Win", 4 + half), ("hT", s, b)], writes=[("pV", half)])
                dst = vo[b % 2][:, half * 512:(half + 1) * 512]
                if half == 0:
                    P.act(lambda e, dst=dst: e.copy(dst, pV[0][:]), reads=[("pV", 0)], writes=[("vo", b % 2, 0)])
                else:
                    P.dve(lambda e, dst=dst: e.tensor_copy(dst, pV[1][:]), reads=[("pV", 1)], writes=[("vo", b % 2, 1)])
            r0 = t * TT + b * 128
            P.dma("sp", vs.ap()[r0:r0 + 128, :], vo[b % 2][:], reads=[("vo", b % 2, 0), ("vo", b % 2, 1)],
                  writes=[("vs", t, b)])

    load(0)
    load(1)
    norm(0)
    for t in range(NT):
        if t + 1 < NT:
            norm(t + 1)
        proj(t)
        if t + 2 < NT:
            load(t + 2)
    P.emit()
    C.close()
    return P.stats


LAM_INIT = 0.8 - 0.6 * float(np.exp(-0.3 * 1))
NEG = -30000.0


def consts_diff():
    c = {}
    bd = np.zeros((8, 128, 640), np.float32)
    ct = np.zeros((128, 8, 32), np.float32)
    kr = np.arange(128)[:, None]
    u = np.arange(640)[None, :]
    for h in range(8):
        slope = 2.0 ** (-(h + 1))
        b = -slope * (u - kr).astype(np.float32)
        diag = np.where((kr // 64) <= (u // 64), -slope * np.abs(u - kr), NEG)
        b = np.where(u < 128, diag, b)
        bd[h] = b
        for n in range(1, 32):
            ct[:, h, n] = -slope * 128.0 * (n - 1)
    c["c_bd"] = np.ascontiguousarray(bd.transpose(1, 0, 2))
    pos = np.arange(T)
    pr_, pb_ = (pos % 128).astype(np.float32), (pos // 128).astype(np.float32)
    posQ = np.stack([-pr_, -128.0 * pb_, np.ones(T, np.float32), np.ones(T, np.float32)], axis=0)
    posK = np.zeros((8, 4, T), np.float32)
    for h in range(8):
        slope = 2.0 ** (-(h + 1))
        posK[h] = slope * np.stack([np.ones(T, np.float32), np.ones(T, np.float32), pr_, 128.0 * pb_], axis=0)
    assert np.array_equal(posQ.astype(ml_dtypes.bfloat16).astype(np.float32), posQ)
    assert np.array_equal(posK.astype(ml_dtypes.bfloat16).astype(np.float32), posK)
    c["c_posQ"] = posQ.astype(ml_dtypes.bfloat16)
    c["c_posK"] = posK.astype(ml_dtypes.bfloat16)
    c["c_ones"] = np.ones((128, 128), ml_dtypes.bfloat16)
    ob = np.zeros((128, 128), np.float32)
    ob[:64, :64] = 1
    ob[64:, 64:] = 1
    c["c_onesbd"] = ob.astype(ml_dtypes.bfloat16)
    return c


def phase_diffattn(nc, dram, qTs, kTs, vs, aoT):
    C = Ctx(nc)
    P = Prog(nc)
    bd = C.sb([128, 8, 640], F32)
    ones = C.sb([128, 128], BF16)
    P.dma("sp", bd[:], dram["c_bd"].ap(), writes=["bd"])
    P.dma("sp", ones[:], dram["c_ones"].ap(), writes=["ones"])
    lv = {}
    for nm in ("od_lam_q1", "od_lam_k1", "od_lam_q2", "od_lam_k2"):
        lv[nm] = C.sb([128, 64], F32)
        P.dma("sp", lv[nm][:], dram[nm].ap()[0].partition_broadcast(128), writes=[nm])
    pr = [C.sb([128, 64], F32) for _ in range(2)]
    sm = [C.sb([128, 1], F32) for _ in range(2)]
    nlam = C.sb([128, 1], F32)
    for i, (a, b) in enumerate((("od_lam_q1", "od_lam_k1"), ("od_lam_q2", "od_lam_k2"))):
        P.dve(lambda e, i=i, a=a, b=b: e.tensor_tensor(out=pr[i][:], in0=lv[a][:], in1=lv[b][:], op=ALU.mult),
              reads=[a, b], writes=[("pr", i)])
        P.dve(lambda e, i=i: e.reduce_sum(sm[i][:], pr[i][:], axis=AX.X), reads=[("pr", i)], writes=[("sm", i)])
        P.act(lambda e, i=i: e.activation(out=sm[i][:], in_=sm[i][:], func=AF.Exp), reads=[("sm", i)], writes=[("sm", i)])
    P.dve(lambda e: e.tensor_tensor(out=nlam[:], in0=sm[1][:], in1=sm[0][:], op=ALU.subtract),
          reads=[("sm", 0), ("sm", 1)], writes=["nlam"])
    P.dve(lambda e: e.tensor_scalar(nlam[:], nlam[:], -LAM_INIT, None, op0=ALU.add), reads=["nlam"], writes=["nlam"])
    wsub = load_col(P, C, dram["od_sub_norm"].ap()[0], 128, 1, "wsub", scale=1.0 - LAM_INIT)

    qT = [[C.sb([68, T], BF16) for _ in range(2)] for _ in range(2)]
    kT = [[C.sb([68, T], BF16) for _ in range(2)] for _ in range(2)]
    for s_ in range(2):
        for m_ in range(2):
            P.dma("sp", qT[s_][m_][64:68, :], dram["c_posQ"].ap(), writes=[("qpos", s_, m_)])
    V = [C.sb([128, NB, 128], BF16) for _ in range(2)]
    tt = [C.sb([128, 2, 512], F32) for _ in range(6)]
    Pm = [C.sb([128, 2, 512], BF16) for _ in range(6)]
    rd = [C.sb([128, 512], F32) for _ in range(2)]
    o12 = [C.sb([128, 512], F32) for _ in range(2)]
    oo = C.sb([128, 512], F32)
    sq = C.sb([128, 512], BF16)
    sd = C.sb([128, 512], F32)
    ob = [C.sb([128, 512], BF16) for _ in range(2)]
    pS = [C.ps([128, 2, 512], F32) for _ in range(2)]
    pO = [C.ps([128, 512], F32) for _ in range(2)]
    pD = [C.ps([128, 512], F32) for _ in range(2)]

    def load_head(h):
        s = h % 2
        for m in range(2):
            P.dma("sp", qT[s][m][0:64, :], qTs.ap()[h, m * 64:(m + 1) * 64, :], writes=[("qT", s, m)])
            P.dma("sp", kT[s][m][0:64, :], kTs.ap()[h, m * 64:(m + 1) * 64, :], writes=[("kT", s, m)])
            P.dma("sp", kT[s][m][64:68, :], dram["c_posK"].ap()[h], writes=[("kpos", s, m)])
        P.dma("sp", V[s][:], vs.ap()[:, h * 128:(h + 1) * 128].rearrange("(j p) d -> p j d", p=128),
              writes=[("V", s)])

    steps = [(h, Q, j) for h in range(8) for Q in range(8) for j in range(4 * Q + 4)]

    def geom(i):
        h, Q, j = steps[i]
        jj = j - 4 * Q
        c0 = jj * 128 if jj > 0 else 0
        return h, Q, j, c0

    def S(i):
        h, Q, j, c0 = geom(i)
        s = h % 2
        full = j < 4 * Q
        kk = 68 if full else 64
        for m in range(2):
            P.pe(lambda e, m=m: e.matmul(pS[i % 2][:, m, c0:512], kT[s][m][0:kk, j * 128:(j + 1) * 128],
                                         qT[s][m][0:kk, Q * 512 + c0:(Q + 1) * 512], start=True, stop=True),
                 reads=[("kT", s, m), ("qT", s, m), ("kpos", s, m), ("qpos", s, m)], writes=[("pS", i % 2)])

    def soft(i):
        h, Q, j, c0 = geom(i)
        full = j < 4 * Q
        N = 512 - c0
        if full:
            P.act(lambda e: e.activation(out=Pm[i % 6][:], in_=pS[i % 2][:], func=AF.Exp),
                  reads=[("pS", i % 2)], writes=[("P", i % 6)])
        else:
            P.dve(lambda e: e.tensor_tensor(out=tt[i % 6][:, :, c0:512], in0=pS[i % 2][:, :, c0:512],
                                            in1=bd[:, h, 0:N].unsqueeze(1).broadcast_to([128, 2, N]), op=ALU.add),
                  reads=[("pS", i % 2), "bd"], writes=[("t", i % 6)])
            P.act(lambda e: e.activation(out=Pm[i % 6][:, :, c0:512], in_=tt[i % 6][:, :, c0:512], func=AF.Exp),
                  reads=[("t", i % 6)], writes=[("P", i % 6)])

    def pv(i):
        h, Q, j, c0 = geom(i)
        s = h % 2
        last = 4 * Q + 3
        for m in range(2):
            P.pe(lambda e, m=m: e.matmul(pO[m][:, c0:512], V[s][:, j, :], Pm[i % 6][:, m, c0:512],
                                         start=(j == 0), stop=(j == last)),
                 reads=[("V", s), ("P", i % 6)], writes=[("pO", m)])
        for m in range(2):
            P.pe(lambda e, m=m: e.matmul(pD[m][:, c0:512], ones[:], Pm[i % 6][:, m, c0:512],
                                         start=(j == 0), stop=(j == last)),
                 reads=["ones", ("P", i % 6)], writes=[("pD", m)])

    ecnt = [0]

    def epilogue(h, Q):
        k = ecnt[0] % 2
        ecnt[0] += 1
        for m in range(2):
            P.dve(lambda e, m=m: e.tensor_copy(o12[m][:], pO[m][:]), reads=[("pO", m)], writes=[("o12", m)])
            P.act(lambda e, m=m: e.activation(out=rd[m][:], in_=pD[m][:], func=AF.Ln), reads=[("pD", m)], writes=[("rd", m)])
        for m in range(2):
            P.act(lambda e, m=m: e.activation(out=rd[m][:], in_=rd[m][:], func=AF.Exp, scale=-1.0),
                  reads=[("rd", m)], writes=[("rd", m)])
            P.dve(lambda e, m=m: e.tensor_tensor(out=o12[m][:], in0=o12[m][:], in1=rd[m][:], op=ALU.mult),
                  reads=[("o12", m), ("rd", m)], writes=[("o12", m)])
        P.dve(lambda e: e.scalar_tensor_tensor(out=oo[:], in0=o12[1][:], scalar=nlam[:, 0:1], in1=o12[0][:],
                                              op0=ALU.mult, op1=ALU.add),
              reads=[("o12", 0), ("o12", 1), "nlam"], writes=["oo"])
        P.act(lambda e: e.activation(out=sq[:], in_=oo[:], func=AF.Square), reads=["oo"], writes=["sq"])
        P.pe(lambda e: e.matmul(pD[0][:], ones[:], sq[:], start=True, stop=True),
             reads=["ones", "sq"], writes=[("pD", 0)])
        P.act(lambda e: e.activation(out=sd[:], in_=pD[0][:], func=AF.Ln, scale=1.0 / 128, bias=EPS),
              reads=[("pD", 0)], writes=["sd"])
        P.act(lambda e: e.activation(out=sd[:], in_=sd[:], func=AF.Exp, scale=-0.5), reads=["sd"], writes=["sd"])
        P.dve(lambda e: e.scalar_tensor_tensor(out=ob[k][:], in0=oo[:], scalar=wsub[:, 0:1], in1=sd[:],
                                              op0=ALU.mult, op1=ALU.mult),
              reads=["oo", "sd", "wsub"], writes=[("ob", k)])
        P.dma("sp", aoT.ap()[h * 128:(h + 1) * 128, Q * 512:(Q + 1) * 512], ob[k][:], reads=[("ob", k)],
              writes=[("aoT", h, Q)])

    load_head(0)
    load_head(1)
    S(0)
    for i in range(len(steps)):
        h, Q, j, c0 = geom(i)
        if i + 1 < len(steps):
            S(i + 1)
        soft(i)
        pv(i)
        if j == 4 * Q + 3:
            epilogue(h, Q)
            if Q == 7 and h + 2 < 8:
                load_head(h + 2)
    P.emit()
    C.close()
    return P.stats


def host_consts_all():
    c = host_consts()
    c.update(consts_diff())
    return c


def scratch(nc):
    s = {}
    s["r1"] = nc.dram_tensor("r1", [T, DM], F32, kind="Internal")
    s["r2"] = nc.dram_tensor("r2", [T, DM], F32, kind="Internal")
    s["qTs"] = nc.dram_tensor("qTs", [8, 128, T], BF16, kind="Internal")
    s["kTs"] = nc.dram_tensor("kTs", [8, 128, T], BF16, kind="Internal")
    s["vs"] = nc.dram_tensor("vs", [T, DM], BF16, kind="Internal")
    s["aoT"] = nc.dram_tensor("aoT", [DM, T], BF16, kind="Internal")
    return s


def consts_swa():
    c = {}
    kr = np.arange(128)[:, None]
    qr = np.arange(128)[None, :]
    cur = np.where((kr // 64) <= (qr // 64), np.abs(qr - kr), 60000.0)
    prev = np.where((qr // 64 == 1) & (kr // 64 == 0), 60000.0, qr + 128 - kr)
    c["c_distm"] = np.ascontiguousarray(np.stack([prev, cur], axis=1).astype(np.float32))
    sw = np.zeros((128, 128), np.float32)
    for i in range(128):
        sw[i, (i + 64) % 128] = 1.0
    c["c_swap"] = sw.astype(ml_dtypes.bfloat16)
    return c


def qk_norm_chunk(P, pQ, pQkey, sq, sqkey, pS, pSkey, sd, sdkey, onesbd, wcol, wkey, out_ap, outkey, nfeat=64,
                  qs=None, qskey=None):
    if qs is not None:
        P.dve(lambda e, src_=pQ: e.tensor_copy(qs[:], src_[:]), reads=[pQkey], writes=[qskey])
        pQ, pQkey = qs, qskey
    P.act(lambda e: e.activation(out=sq[:], in_=pQ[:], func=AF.Square), reads=[pQkey], writes=[sqkey])
    P.pe(lambda e: e.matmul(pS[:], onesbd[:], sq[:], start=True, stop=True), reads=["onesbd", sqkey], writes=[pSkey])
    P.act(lambda e: e.activation(out=sd[:], in_=pS[:], func=AF.Ln, scale=1.0 / nfeat, bias=EPS),
          reads=[pSkey], writes=[sdkey])
    P.act(lambda e: e.activation(out=sd[:], in_=sd[:], func=AF.Exp, scale=-0.5), reads=[sdkey], writes=[sdkey])
    P.dve(lambda e: e.scalar_tensor_tensor(out=out_ap, in0=pQ[:], scalar=wcol[:, 0:1], in1=sd[:],
                                           op0=ALU.mult, op1=ALU.mult),
          reads=[pQkey, sdkey, wkey], writes=[outkey])


def phase_swa(nc, dram, src, yT):
    C = Ctx(nc)
    P = Prog(nc)
    TT = 512
    NT = T // TT
    win = dram["ev_w_in"].ap()[0]
    W = C.sb([128, 8, 1536], BF16)
    nw = C.sb([128, DM], F32)
    ident = load_consts(P, C, dram)
    onesbd = C.sb([128, 128], BF16)
    ones = C.sb([128, 128], BF16)
    swapm = C.sb([128, 128], BF16)
    distm = C.sb([128, 2, 128], F32)
    esk = C.sb([128, 16], F32)
    P.dma("sp", onesbd[:], dram["c_onesbd"].ap(), writes=["onesbd"])
    P.dma("sp", ones[:], dram["c_ones"].ap(), writes=["ones"])
    P.dma("sp", swapm[:], dram["c_swap"].ap(), writes=["swapm"])
    P.dma("sp", distm[:], dram["c_distm"].ap(), writes=["distm"])
    P.dma("sp", nw[:], dram["ev_norm_w"].ap()[0].partition_broadcast(128), writes=["nw"])
    P.dma("sp", esk[:], dram["ev_sinks"].ap()[0].partition_broadcast(128), writes=["esk"])
    P.act(lambda e: e.activation(out=esk[:], in_=esk[:], func=AF.Exp), reads=["esk"], writes=["esk"])
    wq = load_col(P, C, dram["ev_q_norm"].ap()[0], 64, 2, "wq", scale=0.125)
    wk = load_col(P, C, dram["ev_k_norm"].ap()[0], 64, 2, "wk")
    for i in range(3):
        P.dma("pool", W[:, :, i * 512:(i + 1) * 512],
              win.rearrange("(c p) n -> p c n", p=128)[:, :, 2576 + i * 512:2576 + (i + 1) * 512], writes=[("W", i)])
    xa = [C.sb([128, DM], F32) for _ in range(2)]
    hb = [C.sb([128, DM], BF16) for _ in range(2)]
    hT = [C.sb([128, 8, TT], BF16) for _ in range(2)]
    tmps = [dict(jk=C.sb([128, DM], BF16), ssq=C.sb([128, 1], F32), std=C.sb([128, 1], F32), key=i) for i in range(2)]
    sq = [C.sb([128, TT], BF16) for _ in range(2)]
    sd = [C.sb([128, TT], F32) for _ in range(2)]
    qn = C.sb([128, 8, TT], BF16)
    qs3 = [C.sb([128, TT], F32) for _ in range(3)]
    kn = [C.sb([128, 2, TT], BF16) for _ in range(2)]
    knS = [C.sb([128, 2, TT], BF16) for _ in range(2)]
    Vt = [C.sb([128, 4, 65], BF16) for _ in range(2)]
    for i_ in range(2):
        P.pool(lambda e, i_=i_: e.memset(Vt[i_][:, :, 64:65], 1.0), writes=[("Vt1", i_)])
    ytok = [C.sb([128, DM], BF16) for _ in range(2)]
    rd4 = [C.sb([128, 4], F32) for _ in range(2)]
    tt = [[C.sb([128, 512], F32) for _ in range(2)] for _ in range(4)]
    Pm = [[C.sb([128, 512], BF16) for _ in range(2)] for _ in range(4)]
    ysw = [C.sb([128, 8, TT], BF16) for _ in range(2)]
    pT = C.ps([128, DM], BF16)
    F = [C.ps([128, 512], F32) for _ in range(7)]
    srca = src.ap()

    def norm(t):
        s = t % 2
        for b in range(4):
            g = 4 * t + b
            P.dma("sp", xa[g % 2][:], srca[g * 128:(g + 1) * 128, :], writes=[("xa", g % 2)])
            rmsnorm_block(P, xa[g % 2][:], ("xa", g % 2), nw, ident, hb[g % 2], ("hb", g % 2), pT, "pT",
                          hT[s][:, :, b * 128:(b + 1) * 128], ("hT", s, b), tmps[g % 2],
                          "act" if b % 2 == 0 else "dve")

    cnt = [0]

    def proj(t):
        s = t % 2
        hkeys = [("hT", s, b) for b in range(4)]
        for ch in range(10):
            i = cnt[0] % 2
            cnt[0] += 1
            for c in range(8):
                P.pe(lambda e, c=c, ch=ch, i=i: e.matmul(F[i][:], W[:, c, ch * 128:(ch + 1) * 128], hT[s][:, c, :],
                                                        start=(c == 0), stop=(c == 7)),
                     reads=[("W", (ch * 128) // 512)] + hkeys, writes=[("F", i)])
            if ch < 8:
                out_ap, outkey, wcol, wkey = qn[:, ch, :], ("qn", ch), wq, "wq"
            else:
                out_ap, outkey, wcol, wkey = kn[s][:, ch - 8, :], ("kn", s, ch - 8), wk, "wk"
            k3 = (cnt[0] - 1) % 3
            qk_norm_chunk(P, F[i], ("F", i), sq[i], ("sq", i), F[2 + i], ("F", 2 + i), sd[i], ("sd", i), onesbd,
                          wcol, wkey, out_ap, outkey, qs=qs3[k3], qskey=("qs3", k3))
        for ck in range(2):
            P.pe(lambda e, ck=ck: e.matmul(F[2 + ck][:], swapm[:], kn[s][:, ck, :], start=True, stop=True),
                 reads=["swapm", ("kn", s, ck)], writes=[("F", 2 + ck)])
            P.act(lambda e, ck=ck: e.copy(knS[s][:, ck, :], F[2 + ck][:]), reads=[("F", 2 + ck)],
                  writes=[("knS", s, ck)])

    def block(t, b):
        s = t % 2
        g = 4 * t + b
        cb = slice(b * 128, (b + 1) * 128)
        for c in range(8):
            P.pe(lambda e, c=c: e.matmul(F[6][:, 0:256], hT[s][:, c, cb], W[:, c, 1280:1536],
                                         start=(c == 0), stop=(c == 7)),
                 reads=[("W", 2), ("hT", s, b)], writes=[("F", 6)])
        P.dve(lambda e: e.tensor_copy(Vt[g % 2][:, :, 0:64], F[6][:, 0:256].rearrange("p (j d) -> p j d", j=4)),
              reads=[("F", 6)], writes=[("Vt", g % 2)])
        kts = [1] if g == 0 else [0, 1]
        c0 = 256 if g == 0 else 0

        def ksrc(kt, ck):
            if kt == 1:
                return kn[s], knS[s], cb, [("kn", s, ck), ("knS", s, ck)]
            if b > 0:
                return kn[s], knS[s], slice((b - 1) * 128, b * 128), [("kn", s, ck), ("knS", s, ck)]
            return kn[1 - s], knS[1 - s], slice(384, 512), [("kn", 1 - s, ck), ("knS", 1 - s, ck)]

        def do_jv(jv, half2):
            if True:
                ck, hf = jv // 2, jv % 2
                banks = (F[2 * (jv % 2)], F[2 * (jv % 2) + 1])
                bkeys = (("F", 2 * (jv % 2)), ("F", 2 * (jv % 2) + 1))
                for kt in kts:
                    kA, kB, kcols, kkeys = ksrc(kt, ck)
                    for r in range(4):
                        qh = r % 2
                        cq = 2 * jv + r // 2
                        ksrc_t = kA if hf == qh else kB
                        col = (kt * 2 + r // 2) * 128
                        P.pe(lambda e, qh=qh, cq=cq, ksrc_t=ksrc_t, col=col, kcols=kcols: e.matmul(
                            banks[qh][:, col:col + 128], ksrc_t[qh * 64:(qh + 1) * 64, ck, kcols],
                            qn[qh * 64:(qh + 1) * 64, cq, cb], start=True, stop=True),
                            reads=kkeys + [("qn", cq)], writes=[bkeys[qh]])
                pi = jv % 4
                for qh in range(2):
                    t_ = tt[pi][qh]
                    for rr in range(2):
                        hq = 4 * jv + 2 * rr + qh
                        nslope = -(2.0 ** (-(hq + 1) / 2.0))
                        ov = t_[:].rearrange("p (k r q) -> p k r q", k=2, r=2)[:, kts[0]:2, rr, :]
                        bv = banks[qh][:].rearrange("p (k r q) -> p k r q", k=2, r=2)[:, kts[0]:2, rr, :]
                        P.dve(lambda e, ov=ov, bv=bv, nslope=nslope: e.scalar_tensor_tensor(
                            out=ov, in0=distm[:, kts[0]:2, :], scalar=float(nslope), in1=bv, op0=ALU.mult, op1=ALU.add),
                            reads=["distm", bkeys[qh]], writes=[("tt", pi, qh)])
                    P.act(lambda e, t_=t_, pi=pi, qh=qh: e.activation(out=Pm[pi][qh][:, c0:512], in_=t_[:, c0:512],
                                                                      func=AF.Exp),
                          reads=[("tt", pi, qh)], writes=[("Pm", pi, qh)])
                ob_ = F[4 + jv % 2]
                okey = ("F", 4 + jv % 2)
                for r in range(4):
                    qh = r % 2
                    for kt in kts:
                        vsrc = Vt[g % 2] if kt == 1 else Vt[(g - 1) % 2]
                        vkeys = [("Vt", g % 2), ("Vt1", g % 2)] if kt == 1 else [("Vt", (g - 1) % 2), ("Vt1", (g - 1) % 2)]
                        col = (kt * 2 + r // 2) * 128
                        P.pe(lambda e, vsrc=vsrc, qh=qh, r=r, col=col, kt=kt: e.matmul(
                            ob_[:, r * 65:(r + 1) * 65], Pm[pi][qh][:, col:col + 128], vsrc[:, jv, :],
                            start=(kt == kts[0]), stop=(kt == 1)),
                            reads=vkeys + [("Pm", pi, qh)], writes=[okey])
                o3 = ob_[:, 0:260].rearrange("p (r d) -> p r d", r=4)
                rdj = rd4[jv % 2]
                P.dve(lambda e: e.tensor_tensor(out=rdj[:], in0=o3[:, :, 64], in1=esk[:, 4 * jv:4 * jv + 4], op=ALU.add),
                      reads=[okey, "esk"], writes=[("rd4", jv % 2)])
                P.act(lambda e: e.activation(out=rdj[:], in_=rdj[:], func=AF.Ln), reads=[("rd4", jv % 2)], writes=[("rd4", jv % 2)])
                P.act(lambda e: e.activation(out=rdj[:], in_=rdj[:], func=AF.Exp, scale=-1.0), reads=[("rd4", jv % 2)],
                      writes=[("rd4", jv % 2)])
                P.dve(lambda e: e.tensor_tensor(
                    out=ytok[g % 2][:, jv * 256:(jv + 1) * 256].rearrange("p (r d) -> p r d", r=4), in0=o3[:, :, 0:64],
                    in1=rdj[:].unsqueeze(2).broadcast_to([128, 4, 64]), op=ALU.mult),
                    reads=[okey, ("rd4", jv % 2)], writes=[("ytok", g % 2, jv)])

        for half2 in range(2):
            for jv in (2 * half2, 2 * half2 + 1):
                do_jv(jv, half2)
        for ch in range(8):
            P.pe(lambda e, ch=ch: e.transpose(pT[:, ch * 128:(ch + 1) * 128], ytok[g % 2][:, ch * 128:(ch + 1) * 128], ident[:]),
                 reads=[("ytok", g % 2, ch // 2), "ident"], writes=["pT"])
        P.act(lambda e: e.copy(ysw[s][:, :, cb], pT[:].rearrange("p (c t) -> p c t", c=8)), reads=["pT"],
              writes=[("ysw", s, b)])

    def store(t):
        s = t % 2
        P.dma("sp", yT.ap()[1024:2048, t * TT:(t + 1) * TT].rearrange("(c p) t -> p c t", p=128), ysw[s][:],
              reads=[("ysw", s, b) for b in range(4)], writes=[("yT", t)])

    for t in range(NT):
        norm(t)
        proj(t)
        for b in range(4):
            block(t, b)
        store(t)
    P.emit()
    C.close()
    return P.stats


def host_layout_inputs(inp):
    d = {}
    cw = np.asarray(inp["ev_conv_w"])[0]
    d["ev_conv_wT"] = np.ascontiguousarray(cw.reshape(4, 12, 128).transpose(2, 1, 0))
    cb = np.asarray(inp["ev_conv_b"])[0]
    d["ev_conv_bT"] = np.ascontiguousarray(cb.reshape(12, 128).transpose(1, 0))
    sk = np.asarray(inp["ev_sinks"])[0]
    d["ev_sinksT"] = np.ascontiguousarray(sk.reshape(8, 2).transpose(1, 0))
    return d


IN_SHAPES["ev_sinksT"] = [2, 8]


def consts_ssd():
    c = {}
    i = np.arange(128)
    c["c_U"] = (i[:, None] <= i[None, :]).astype(np.float32)
    c["c_onesf"] = np.ones((128, 128), np.float32)
    c["c_maskneg"] = np.where(i[:, None] <= i[None, :], 0.0, NEG).astype(np.float32)
    return c


def phase_ssd(nc, dram, src, yT):
    C = Ctx(nc)
    P = Prog(nc)
    TT = 512
    NT = T // TT
    win = dram["ev_w_in"].ap()[0]
    W = C.sb([128, 8, 2576], BF16)
    nw = C.sb([128, DM], F32)
    ssdw = C.sb([128, DM], F32)
    cw = C.sb([128, 12, 4], F32)
    cbias = C.sb([128, 12], F32)
    dtb = C.sb([128, 16], F32)
    aneg = C.sb([128, 16], F32)
    dsk = C.sb([128, 16], F32)
    U = C.sb([128, 128], F32)
    onesf = C.sb([128, 128], F32)
    maskneg = C.sb([128, 128], F32)
    ident = load_consts(P, C, dram)
    P.dma("sp", nw[:], dram["ev_norm_w"].ap()[0].partition_broadcast(128), writes=["nw"])
    P.dma("sp", ssdw[:], dram["ev_ssd_norm_w"].ap()[0].partition_broadcast(128), writes=["ssdw"])
    P.dma("sp", cw[:], dram["ev_conv_wT"].ap(), writes=["cw"])
    P.dma("sp", cbias[:], dram["ev_conv_bT"].ap(), writes=["cbias"])
    P.dma("sp", dtb[:], dram["ev_dt_bias"].ap()[0].partition_broadcast(128), writes=["dtb"])
    P.dma("sp", aneg[:], dram["ev_a_log"].ap()[0].partition_broadcast(128), writes=["aneg"])
    P.dma("sp", dsk[:], dram["ev_d_skip"].ap()[0].partition_broadcast(128), writes=["dsk"])
    P.dma("sp", U[:], dram["c_U"].ap(), writes=["U"])
    P.dma("sp", onesf[:], dram["c_onesf"].ap(), writes=["onesf"])
    P.dma("sp", maskneg[:], dram["c_maskneg"].ap(), writes=["maskneg"])
    P.act(lambda e: e.activation(out=aneg[:], in_=aneg[:], func=AF.Exp), reads=["aneg"], writes=["aneg"])
    P.dve(lambda e: e.tensor_scalar(aneg[:], aneg[:], -1.0, None, op0=ALU.mult), reads=["aneg"], writes=["aneg"])
    wblocks = [(i * 512, (i + 1) * 512) for i in range(5)] + [(2560, 2576)]
    for i, (a, b) in enumerate(wblocks):
        P.dma("pool", W[:, :, a:b], win.rearrange("(c p) n -> p c n", p=128)[:, :, a:b], writes=[("W", i)])

    def wkeys(a, b):
        return [("W", i) for i, (x, y) in enumerate(wblocks) if x < b and y > a]

    xa = [C.sb([128, DM], F32) for _ in range(2)]
    hb = [C.sb([128, DM], BF16) for _ in range(2)]
    hT = [C.sb([128, 8, TT], BF16) for _ in range(2)]
    tmps = [dict(jk=C.sb([128, DM], BF16), ssq=C.sb([128, 1], F32), std=C.sb([128, 1], F32), key=i) for i in range(2)]
    xp = C.sb([128, 12, TT + 3], F32)
    xcs = C.sb([128, 12, TT], BF16)
    acc = [C.sb([128, TT], F32) for _ in range(2)]
    xs_tok_2 = [C.sb([128, DM], BF16) for _ in range(2)]
    B_tok_2 = [C.sb([128, 256], BF16) for _ in range(2)]
    zs_2 = [C.sb([128, DM], F32) for _ in range(2)]
    xdt_2 = [C.sb([128, DM], BF16) for _ in range(2)]
    xw_2 = [C.sb([128, DM], BF16) for _ in range(2)]
    D2 = C.sb([128, 8, 128], F32)
    sg_2 = [C.sb([128, 8, 128], F32) for _ in range(2)]
    M = [C.sb([128, 8, 128], BF16) for _ in range(2)]
    y_2 = [C.sb([128, DM], F32) for _ in range(2)]
    tmpf = C.sb([128, DM], F32)
    ysb_2 = [C.sb([128, DM], BF16) for _ in range(2)]
    S = C.sb([128, DM], F32)
    Sb = C.sb([128, DM], BF16)
    yTt = [C.sb([128, 8, TT], BF16) for _ in range(2)]
    sm = {k: C.sb([128, 4, 16], F32) for k in ("u", "au", "l", "dt", "adt", "acs", "ea", "dte", "etot", "w")}
    gss = C.sb([128, 2], F32)
    cbs_2 = [C.sb([128, 256], F32) for _ in range(2)]
    pT = C.ps([128, DM], BF16)
    F = [C.ps([128, 512], F32) for _ in range(7)]
    srca = src.ap()

    P.pool(lambda e: e.memset(xp[:, :, 0:3], 0.0), writes=[("xp", ch) for ch in range(12)])
    P.pool(lambda e: e.memset(S[:], 0.0), writes=["S"])
    P.pool(lambda e: e.memset(Sb[:], 0.0), writes=["Sb"])

    def norm(t):
        s = t % 2
        for b in range(4):
            g = 4 * t + b
            P.dma("sp", xa[g % 2][:], srca[g * 128:(g + 1) * 128, :], writes=[("xa", g % 2)])
            rmsnorm_block(P, xa[g % 2][:], ("xa", g % 2), nw, ident, hb[g % 2], ("hb", g % 2), pT, "pT",
                          hT[s][:, :, b * 128:(b + 1) * 128], ("hT", s, b), tmps[g % 2],
                          "act" if b % 2 == 0 else "dve")

    def proj_conv(t):
        s = t % 2
        hkeys = [("hT", s, b) for b in range(4)]

        def chunk(ch):
            i = ch % 2
            c0 = 1024 + ch * 128
            for c in range(8):
                P.pe(lambda e, c=c: e.matmul(F[i][:], W[:, c, c0:c0 + 128], hT[s][:, c, :], start=(c == 0), stop=(c == 7)),
                     reads=wkeys(c0, c0 + 128) + hkeys, writes=[("F", i)])
            if i == 0:
                P.act(lambda e: e.copy(xp[:, ch, 3:TT + 3], F[i][:]), reads=[("F", i)], writes=[("xp", ch)])
            else:
                P.dve(lambda e: e.tensor_copy(xp[:, ch, 3:TT + 3], F[i][:]), reads=[("F", i)], writes=[("xp", ch)])
            a = acc[i]
            P.act(lambda e: e.activation(out=a[:], in_=xp[:, ch, 3:TT + 3], func=AF.Identity,
                                         scale=cw[:, ch, 3:4], bias=cbias[:, ch:ch + 1]),
                  reads=[("xp", ch), "cw", "cbias"], writes=[("acc", i)])
            for k in (2, 1, 0):
                P.dve(lambda e, k=k: e.scalar_tensor_tensor(out=a[:], in0=xp[:, ch, k:TT + k], scalar=cw[:, ch, k:k + 1],
                                                            in1=a[:], op0=ALU.mult, op1=ALU.add),
                      reads=[("xp", ch), ("acc", i), "cw"], writes=[("acc", i)])
            P.act(lambda e: e.activation(out=xcs[:, ch, :], in_=a[:], func=AF.Silu), reads=[("acc", i)],
                  writes=[("xcs", ch)])
            P.act(lambda e: e.copy(xp[:, ch, 0:3], xp[:, ch, TT:TT + 3]), reads=[("xp", ch)], writes=[("xp", ch)])

        for ch in range(12):
            chunk(ch)

    def dt_tile(t):
        s = t % 2
        u, au, l_, dt, adt, acs, ea, dte, etot, w_ = (sm[k][:].rearrange("p b h -> p (b h)") for k in
                                                      ("u", "au", "l", "dt", "adt", "acs", "ea", "dte", "etot", "w"))
        smk = ["u", "au", "l", "dt", "adt", "acs", "ea", "dte", "etot", "w"]
        for b in range(4):
            cb = slice(b * 128, (b + 1) * 128)
            for c in range(8):
                P.pe(lambda e, c=c, b=b, cb=cb: e.matmul(F[2][:, b * 16:(b + 1) * 16], hT[s][:, c, cb], W[:, c, 2560:2576],
                                                         start=(c == 0), stop=(c == 7)),
                     reads=wkeys(2560, 2576) + [("hT", s, b)], writes=[("F", 2)])
        P.dve(lambda e: e.tensor_tensor(out=sm["u"][:], in0=F[2][:, 0:64].rearrange("p (b h) -> p b h", b=4),
                                        in1=dtb[:].unsqueeze(1).broadcast_to([128, 4, 16]), op=ALU.add),
              reads=[("F", 2), "dtb"], writes=["u"])
        P.dve(lambda e: e.tensor_scalar(au, u, -1.0, None, op0=ALU.mult), reads=["u"], writes=["au"])
        P.dve(lambda e: e.tensor_tensor(out=au, in0=au, in1=u, op=ALU.min), reads=["u", "au"], writes=["au"])
        P.act(lambda e: e.activation(out=l_, in_=au, func=AF.Exp), reads=["au"], writes=["l"])
        P.act(lambda e: e.activation(out=l_, in_=l_, func=AF.Ln, bias=1.0), reads=["l"], writes=["l"])
        P.dve(lambda e: e.scalar_tensor_tensor(out=dt, in0=u, scalar=0.0, in1=l_, op0=ALU.max, op1=ALU.add),
              reads=["u", "l"], writes=["dt"])
        P.dve(lambda e: e.tensor_tensor(out=sm["adt"][:], in0=sm["dt"][:],
                                        in1=aneg[:].unsqueeze(1).broadcast_to([128, 4, 16]), op=ALU.mult),
              reads=["dt", "aneg"], writes=["adt"])
        P.pe(lambda e: e.matmul(F[2][:, 64:128], U[:], adt, start=True, stop=True), reads=["U", "adt"], writes=[("F", 2)])
        P.pe(lambda e: e.matmul(F[2][:, 128:192], onesf[:], adt, start=True, stop=True), reads=["onesf", "adt"],
             writes=[("F", 2)])
        P.act(lambda e: e.activation(out=ea, in_=F[2][:, 64:128], func=AF.Exp), reads=[("F", 2)], writes=["ea"])
        P.dve(lambda e: e.tensor_copy(acs, F[2][:, 64:128]), reads=[("F", 2)], writes=["acs"])
        P.dve(lambda e: e.tensor_tensor(out=dte, in0=F[2][:, 128:192], in1=acs, op=ALU.subtract),
              reads=[("F", 2), "acs"], writes=["dte"])
        P.act(lambda e: e.activation(out=dte, in_=dte, func=AF.Exp), reads=["dte"], writes=["dte"])
        P.act(lambda e: e.activation(out=etot, in_=F[2][:, 128:192], func=AF.Exp), reads=[("F", 2)], writes=["etot"])
        P.dve(lambda e: e.tensor_tensor(out=w_, in0=dt, in1=dte, op=ALU.mult), reads=["dt", "dte"], writes=["w"])

    def block(t, b):
        s = t % 2
        g = 4 * t + b
        cb = slice(b * 128, (b + 1) * 128)
        hk = [("hT", s, b)]
        gp = g % 2
        xs_tok, B_tok, zs, xdt, xw, sg, y, ysb = (xs_tok_2[gp], B_tok_2[gp], zs_2[gp], xdt_2[gp], xw_2[gp],
                                                     sg_2[gp], y_2[gp], ysb_2[gp])
        for ch in range(8):
            P.pe(lambda e, ch=ch: e.transpose(pT[:, ch * 128:(ch + 1) * 128], xcs[:, ch, cb], ident[:]),
                 reads=[("xcs", ch), "ident"], writes=["pT"])
        P.act(lambda e: e.copy(xs_tok[:], pT[:]), reads=["pT"], writes=[("xs_tok", gp)])
        for ch in range(2):
            P.pe(lambda e, ch=ch: e.transpose(pT[:, ch * 128:(ch + 1) * 128], xcs[:, 8 + ch, cb], ident[:]),
                 reads=[("xcs", 8 + ch), "ident"], writes=["pT"])
        P.dve(lambda e: e.tensor_copy(B_tok[:], pT[:, 0:256]), reads=["pT"], writes=[("B_tok", gp)])
        for half in range(2):
            for c in range(8):
                P.pe(lambda e, c=c, half=half: e.matmul(F[half][:], hT[s][:, c, cb], W[:, c, half * 512:(half + 1) * 512],
                                                        start=(c == 0), stop=(c == 7)),
                     reads=wkeys(half * 512, half * 512 + 512) + hk, writes=[("F", half)])
            P.act(lambda e, half=half: e.activation(out=zs[:, half * 512:(half + 1) * 512], in_=F[half][:], func=AF.Silu),
                  reads=[("F", half)], writes=[("zs", gp, half)])
        u, au, l_, dt, adt, acs, ea, dte, etot, w_ = (sm[k][:, b, :] for k in ("u", "au", "l", "dt", "adt", "acs", "ea", "dte", "etot", "w"))
        xs3 = xs_tok[:].rearrange("p (h d) -> p h d", h=16)
        P.dve(lambda e: e.tensor_tensor(out=xdt[:].rearrange("p (h d) -> p h d", h=16), in0=xs3,
                                        in1=dt[:].unsqueeze(2).broadcast_to([128, 16, 64]), op=ALU.mult),
              reads=[("xs_tok", gp), "dt"], writes=[("xdt", gp)])
        P.pool(lambda e: e.tensor_tensor(out=xw[:].rearrange("p (h d) -> p h d", h=16), in0=xs3,
                                         in1=w_[:].unsqueeze(2).broadcast_to([128, 16, 64]), op=ALU.mult),
               reads=[("xs_tok", gp), "w"], writes=[("xw", gp)])
        for gq in range(2):
            P.pe(lambda e, gq=gq: e.matmul(F[2][:, 256 + gq * 128:256 + (gq + 1) * 128], xcs[:, 8 + gq, cb],
                                           xcs[:, 10 + gq, cb], start=True, stop=True),
                 reads=[("xcs", 8 + gq), ("xcs", 10 + gq)], writes=[("F", 2)])
        cbs = cbs_2[gp]
        P.dve(lambda e: e.tensor_copy(cbs[:], F[2][:, 256:512]), reads=[("F", 2)], writes=[("cbs", gp)])
        Fy = (F[6], F[3])

        def half_fn(hh):
            P.pool(lambda e: e.tensor_tensor(out=D2[:], in0=adt[:, 8 * hh:8 * hh + 8].unsqueeze(2).broadcast_to([128, 8, 128]),
                                             in1=U[:].unsqueeze(1).broadcast_to([128, 8, 128]), op=ALU.mult),
                   reads=["adt", "U"], writes=["D2"])
            for q4 in range(2):
                P.pe(lambda e, q4=q4: e.matmul(F[4 + q4][:], onesf[:],
                                               D2[:, 4 * q4:4 * q4 + 4, :].rearrange("p h l -> p (h l)"),
                                               start=True, stop=True),
                     reads=["onesf", "D2"], writes=[("F", 4 + q4)])
                P.dve(lambda e, q4=q4: e.tensor_tensor(
                    out=sg[:, 4 * q4:4 * q4 + 4, :], in0=F[4 + q4][:].rearrange("p (h l) -> p h l", h=4),
                    in1=acs[:, 8 * hh + 4 * q4:8 * hh + 4 * q4 + 4].unsqueeze(2).broadcast_to([128, 4, 128]),
                    op=ALU.subtract),
                    reads=[("F", 4 + q4), "acs"], writes=[("sg", gp)])
            P.dve(lambda e: e.scalar_tensor_tensor(out=sg[:], in0=sg[:], scalar=0.0,
                                                   in1=maskneg[:].unsqueeze(1).broadcast_to([128, 8, 128]),
                                                   op0=ALU.min, op1=ALU.add),
                  reads=[("sg", gp), "maskneg"], writes=[("sg", gp)])
            P.act(lambda e: e.activation(out=sg[:], in_=sg[:], func=AF.Exp), reads=[("sg", gp)], writes=[("sg", gp)])
            P.dve(lambda e: e.tensor_tensor(
                out=M[hh][:], in0=sg[:],
                in1=cbs[:, hh * 128:(hh + 1) * 128].unsqueeze(1).broadcast_to([128, 8, 128]), op=ALU.mult),
                reads=[("sg", gp), ("cbs", gp)], writes=[("M", hh)])
            for hl in range(8):
                h = 8 * hh + hl
                P.pe(lambda e, hl=hl, h=h: e.matmul(Fy[hh][:, hl * 64:(hl + 1) * 64], M[hh][:, hl, :],
                                                    xdt[:, h * 64:(h + 1) * 64], start=True, stop=True),
                     reads=[("M", hh), ("xdt", gp)], writes=[("F", 6 if hh == 0 else 3)])

        for hh in range(2):
            half_fn(hh)
        for gq in range(2):
            P.pe(lambda e, gq=gq: e.matmul(F[gq][:], xcs[:, 10 + gq, cb], Sb[:, gq * 512:(gq + 1) * 512], start=True, stop=True),
                 reads=[("xcs", 10 + gq), "Sb"], writes=[("F", gq)])
        for gq in range(2):
            ysl = y[:, gq * 512:(gq + 1) * 512]
            P.dve(lambda e, gq=gq, ysl=ysl: e.tensor_tensor(
                out=ysl.rearrange("p (h d) -> p h d", h=8), in0=F[gq][:].rearrange("p (h d) -> p h d", h=8),
                in1=ea[:, 8 * gq:8 * gq + 8].unsqueeze(2).broadcast_to([128, 8, 64]), op=ALU.mult),
                reads=[("F", gq), "ea"], writes=[("y", gp, gq)])
            P.dve(lambda e, gq=gq, ysl=ysl: e.tensor_tensor(out=ysl, in0=Fy[gq][:], in1=ysl, op=ALU.add),
                  reads=[("F", 6 if gq == 0 else 3), ("y", gp, gq)], writes=[("y", gp, gq)])
        P.pool(lambda e: e.tensor_tensor(out=tmpf[:].rearrange("p (h d) -> p h d", h=16), in0=xs3,
                                         in1=dsk[:].unsqueeze(2).broadcast_to([128, 16, 64]), op=ALU.mult),
               reads=[("xs_tok", gp), "dsk"], writes=["tmpf"])
        P.pool(lambda e: e.tensor_tensor(out=y[:], in0=y[:], in1=tmpf[:], op=ALU.add),
               reads=[("y", gp, 0), ("y", gp, 1), "tmpf"], writes=[("y", gp, 0), ("y", gp, 1)])
        P.pool(lambda e: e.tensor_tensor(out=y[:], in0=y[:], in1=zs[:], op=ALU.mult),
               reads=[("y", gp, 0), ("y", gp, 1), ("zs", gp, 0), ("zs", gp, 1)], writes=[("y", gp, 0), ("y", gp, 1)])
        for gq in range(2):
            P.act(lambda e, gq=gq: e.activation(out=tmpf[:, gq * 512:(gq + 1) * 512], in_=y[:, gq * 512:(gq + 1) * 512],
                                                func=AF.Square, accum_out=gss[:, gq:gq + 1]),
                  reads=[("y", gp, gq)], writes=["gss", "tmpf"])
        P.act(lambda e: e.activation(out=gss[:], in_=gss[:], func=AF.Ln, scale=1.0 / 512, bias=EPS),
              reads=["gss"], writes=["gss"])
        P.act(lambda e: e.activation(out=gss[:], in_=gss[:], func=AF.Exp, scale=-0.5), reads=["gss"], writes=["gss"])
        for gq in range(2):
            P.dve(lambda e, gq=gq: e.scalar_tensor_tensor(
                out=ysb[:, gq * 512:(gq + 1) * 512], in0=y[:, gq * 512:(gq + 1) * 512], scalar=gss[:, gq:gq + 1],
                in1=ssdw[:, gq * 512:(gq + 1) * 512], op0=ALU.mult, op1=ALU.mult),
                reads=[("y", gp, gq), "gss", "ssdw"], writes=[("ysb", gp, gq)])
        for ch in range(8):
            P.pe(lambda e, ch=ch: e.transpose(pT[:, ch * 128:(ch + 1) * 128], ysb[:, ch * 128:(ch + 1) * 128], ident[:]),
                 reads=[("ysb", gp, ch // 4), "ident"], writes=["pT"])
        P.act(lambda e: e.copy(yTt[s][:, :, cb], pT[:].rearrange("p (c t) -> p c t", c=8)), reads=["pT"],
              writes=[("yTt", s, b)])
        for gq in range(2):
            P.pe(lambda e, gq=gq: e.matmul(F[4 + gq][:], B_tok[:, gq * 128:(gq + 1) * 128], xw[:, gq * 512:(gq + 1) * 512],
                                           start=True, stop=True),
                 reads=[("B_tok", gp), ("xw", gp)], writes=[("F", 4 + gq)])
        P.pool(lambda e: e.tensor_tensor(out=S[:].rearrange("p (h d) -> p h d", h=16),
                                         in0=S[:].rearrange("p (h d) -> p h d", h=16),
                                         in1=etot[:].unsqueeze(2).broadcast_to([128, 16, 64]), op=ALU.mult),
               reads=["S", "etot"], writes=["S"])
        for gq in range(2):
            P.dve(lambda e, gq=gq: e.tensor_tensor(out=S[:, gq * 512:(gq + 1) * 512], in0=F[4 + gq][:],
                                                   in1=S[:, gq * 512:(gq + 1) * 512], op=ALU.add),
                  reads=[("F", 4 + gq), "S"], writes=["S"])
        P.act(lambda e: e.copy(Sb[:], S[:]), reads=["S"], writes=["Sb"])

    def store(t):
        s = t % 2
        P.dma("sp", yT.ap()[0:1024, t * TT:(t + 1) * TT].rearrange("(c p) t -> p c t", p=128), yTt[s][:],
              reads=[("yTt", s, b) for b in range(4)], writes=[("yT", t)])

    for t in range(NT):
        norm(t)
        dt_tile(t)
        proj_conv(t)
        for b in range(4):
            block(t, b)
        store(t)
    P.emit()
    C.close()
    return P.stats


IN_SHAPES["ev_conv_wT"] = [128, 12, 4]
IN_SHAPES["ev_conv_bT"] = [128, 12]


def phase_wout(nc, dram, src, yT, dst):
    C = Ctx(nc)
    P = Prog(nc)
    TT = 512
    NT = T // TT
    wo = dram["ev_w_out"].ap()[0]
    Wo = C.sb([128, 16, DM], BF16)
    for i in range(4):
        P.dma("pool", Wo[:, 4 * i:4 * i + 4, :], wo.rearrange("(c p) n -> p c n", p=128)[:, 4 * i:4 * i + 4, :],
              writes=[("Wo", i)])
    at = [C.sb([128, 16, TT], BF16) for _ in range(2)]
    xt = [C.sb([128, 4, DM], F32) for _ in range(2)]
    pY = [C.ps([128, 512], F32) for _ in range(4)]
    srca, dsta = src.ap(), dst.ap()

    def load(t):
        s = t % 2
        P.dma("sp", xt[s][:], srca[t * TT:(t + 1) * TT, :].rearrange("(b p) d -> p b d", p=128),
              writes=[("xt", s, b) for b in range(4)])
        P.dma("sp", at[s][:], yT.ap()[:, t * TT:(t + 1) * TT].rearrange("(c p) t -> p c t", p=128), writes=[("at", s)])

    def comp(t):
        s = t % 2
        for b in range(4):
            for half in range(2):
                k = (b * 2 + half) % 4
                for c in range(16):
                    P.pe(lambda e, c=c, b=b, half=half, k=k: e.matmul(
                        pY[k][:], at[s][:, c, b * 128:(b + 1) * 128], Wo[:, c, half * 512:(half + 1) * 512],
                        start=(c == 0), stop=(c == 15)),
                        reads=[("at", s), ("Wo", c // 4)], writes=[("pY", k)])
                xs = xt[s][:, b, half * 512:(half + 1) * 512]
                P.dve(lambda e, xs=xs, k=k: e.tensor_tensor(out=xs, in0=pY[k][:], in1=xs, op=ALU.add),
                      reads=[("pY", k), ("xt", s, b)], writes=[("xt", s, b)])
        P.dma("sp", dsta[t * TT:(t + 1) * TT, :].rearrange("(b p) d -> p b d", p=128), xt[s][:],
              reads=[("xt", s, b) for b in range(4)], writes=[("dst", t)])

    load(0)
    load(1)
    for t in range(NT):
        comp(t)
        if t + 2 < NT:
            load(t + 2)
    P.emit()
    C.close()
    return P.stats


def all_consts():
    c = host_consts_all()
    c.update(consts_swa())
    c.update(consts_ssd())
    return c


_CACHE = {}


def build_full():
    nc = bass.Bass("TRN2", target_bir_lowering=False)
    consts = all_consts()
    dram = declare(nc, consts)
    out = nc.dram_tensor("out", [T, DM], F32, kind="ExternalOutput")
    S = scratch(nc)
    yT = nc.dram_tensor("yT", [2048, T], BF16, kind="Internal")
    stats = {}
    with nc.named_scope("ssd"):
        stats["ssd"] = phase_ssd(nc, dram, dram["x"], yT)
    with nc.named_scope("swa"):
        stats["swa"] = phase_swa(nc, dram, dram["x"], yT)
    with nc.named_scope("wout"):
        stats["wout"] = phase_wout(nc, dram, dram["x"], yT, S["r1"])
    with nc.named_scope("mlp0"):
        stats["mlp0"] = phase_mlp(nc, dram, S["r1"], S["r2"], 0)
    with nc.named_scope("qkv1"):
        stats["qkv1"] = phase_qkv1(nc, dram, S["r2"], S["qTs"], S["kTs"], S["vs"])
    with nc.named_scope("attn1"):
        stats["attn1"] = phase_diffattn(nc, dram, S["qTs"], S["kTs"], S["vs"], S["aoT"])
    with nc.named_scope("mlp1"):
        stats["mlp1"] = phase_mlp(nc, dram, S["r2"], out, 1, pre=(S["aoT"], dram["od_w_out"].ap()[0]))
    return nc, consts, stats


def kernel(**inputs):
    inp = {k: np.asarray(v) for k, v in inputs.items()}
    if "nc" not in _CACHE:
        _CACHE["nc"], _CACHE["consts"], _CACHE["stats"] = build_full()
    nc, consts = _CACHE["nc"], _CACHE["consts"]
    lay = host_layout_inputs(inp)
    shared = {}
    for k in IN_SHAPES:
        if k == "x":
            continue
        if k in lay:
            shared[k] = lay[k].astype(np.float32)
        else:
            shared[k] = np.ascontiguousarray(inp[k], dtype=np.float32)
    shared.update(consts)
    in_maps = []
    for i in range(NCORES):
        d = dict(shared)
        d["x"] = np.ascontiguousarray(inp["x"][i], dtype=np.float32)
        in_maps.append(d)
    res = run_bass_kernel_spmd(nc, in_maps, core_ids=list(range(NCORES)))
    return np.stack([np.asarray(res.results[i]["out"], dtype=np.float32) for i in range(NCORES)], axis=0)
```

```python
import numpy as np
import concourse.bass as bass
import concourse.mybir as mybir

F32 = mybir.dt.float32
BF16 = mybir.dt.bfloat16
AF = mybir.ActivationFunctionType
ALU = mybir.AluOpType
AX = mybir.AxisListType

SEM_CH = 2000
N_DMA_SEMS = {"sp": 12, "pool": 4, "act": 2}


_PSUM_NAMES = {"F", "pY", "pH", "pT", "pQ", "pS", "pV", "pO", "pD"}


def _is_psum_key(k):
    if isinstance(k, tuple):
        return k[0] in _PSUM_NAMES
    return k in _PSUM_NAMES


class _Op:
    __slots__ = ("eng", "fn", "reads", "writes", "dma", "idx", "seq", "waits", "signal",
                 "sig", "clock", "tab")

    def __init__(self, eng, fn, reads, writes, dma, idx):
        self.eng, self.fn, self.reads, self.writes, self.dma, self.idx = eng, fn, reads, writes, dma, idx
        self.waits = []
        self.signal = False
        self.sig = None
        self.tab = None


SCHED = [True]
SCHED_P = [450.0, 60.0, 400]


class _Rec:
    def __init__(self):
        self.calls = []

    def __getattr__(self, name):
        def f(*a, **k):
            self.calls.append((name, a, k))
            return self
        return f


class SemPool:
    _inst = {}

    @classmethod
    def get(cls, nc):
        if id(nc) not in cls._inst:
            cls._inst[id(nc)] = cls(nc)
        return cls._inst[id(nc)]

    def __init__(self, nc):
        import contextlib
        self.nc = nc
        self.st = contextlib.ExitStack()
        self.esems = {e: [] for e in Prog.ENGS}
        self.sigc = {e: 0 for e in Prog.ENGS}
        self.dsems = {q: [self.st.enter_context(nc.semaphore(f"d_{q}_{j}")) for j in range(n)]
                      for q, n in N_DMA_SEMS.items()}
        self.dcount = {q: [0] * n for q, n in N_DMA_SEMS.items()}
        self.drr = {q: 0 for q in N_DMA_SEMS}

    def eng_sem(self, e, j):
        while len(self.esems[e]) <= j:
            self.esems[e].append(self.st.enter_context(self.nc.semaphore(f"s_{e}_{len(self.esems[e])}")))
        return self.esems[e][j]


class Prog:
    ENGS = ("pe", "act", "dve", "pool", "sp")
    PH = [0]

    def __init__(self, nc):
        Prog.PH[0] += 1
        self.nc = nc
        self.ops = []
        self.last_w = {}
        self.readers = {}

    def add(self, eng, fn, reads=(), writes=(), dma=False):
        idx = len(self.ops)
        op = _Op(eng, fn, tuple(reads), tuple(writes), dma, idx)
        raw, other = set(), set()
        excl = [k for k in op.reads if _is_psum_key(k)]
        for k in op.reads:
            w = self.last_w.get(k)
            if w is not None:
                raw.add(w)
        for k in excl:
            for r in self.readers.get(k, ()):
                other.add(r)
        for k in op.writes:
            w = self.last_w.get(k)
            if w is not None:
                other.add(w)
            for r in self.readers.get(k, ()):
                other.add(r)
        for k in op.reads:
            self.readers.setdefault(k, []).append(idx)
        for k in op.writes:
            self.last_w[k] = idx
            self.readers[k] = []
        for k in excl:
            if k not in op.writes:
                self.last_w[k] = idx
                self.readers[k] = []
        other -= raw
        other.discard(idx)
        raw.discard(idx)
        op.waits = (raw, other)
        self.ops.append(op)
        return op

    def pe(self, fn, reads=(), writes=()):
        return self.add("pe", fn, reads, writes)

    def act(self, fn, reads=(), writes=()):
        return self.add("act", fn, reads, writes)

    def dve(self, fn, reads=(), writes=()):
        return self.add("dve", fn, reads, writes)

    def pool(self, fn, reads=(), writes=()):
        return self.add("pool", fn, reads, writes)

    def dma(self, q, out, in_, reads=(), writes=(), **kw):
        return self.add(q, lambda e: e.dma_start(out=out, in_=in_, **kw), reads, writes, dma=True)

    def _est(self, op):
        rec = _Rec()
        try:
            op.fn(rec)
        except Exception:
            pass
        if not rec.calls:
            return 200.0, 200.0
        name, a, k = rec.calls[0]

        def fsz(ap):
            try:
                shp = ap.shape
                n = 1
                for d in shp[1:]:
                    n *= int(d)
                return n
            except Exception:
                return 256

        if op.dma:
            out = k.get("out", a[0] if a else None)
            n = fsz(out) * 128 * 4
            issue = 1500.0 if op.eng == "pool" else 120.0
            return issue, issue + 2500.0 + n / 120.0
        if op.eng == "pe":
            if name == "transpose":
                return 70.0, 300.0
            rhs = k.get("rhs", a[2] if len(a) > 2 else None)
            n = max(fsz(rhs), 64)
            f = 4.0 if str(getattr(rhs, "dtype", "")).endswith("float32") else 1.0
            d = f * n / 2.3 + 10
            return d, d + 150.0
        out = k.get("out", a[0] if a else None)
        n = fsz(out)
        if op.eng == "act":
            fn_ = str(k.get("func", ""))
            op.tab = "S" if "Silu" in fn_ else ("E" if ("Exp" in fn_ or "Ln" in fn_) else None)
            d = 210.0 + n / 1.25 + (100.0 if k.get("accum_out") is not None else 0.0)
        elif op.eng == "dve":
            d = 80.0 + n / 0.96
        else:
            d = 150.0 + n * 1.2
        return d, d + 100.0

    def schedule(self):
        import heapq
        ops = self.ops
        n = len(ops)
        preds = [set(op.waits[0]) | set(op.waits[1]) for op in ops]
        succs = [[] for _ in range(n)]
        indeg = [0] * n
        for i, ps in enumerate(preds):
            indeg[i] = len(ps)
            for p in ps:
                succs[p].append(i)
        est = [self._est(op) for op in ops]
        cp = [0.0] * n
        for i in range(n - 1, -1, -1):
            m = 0.0
            for j in succs[i]:
                if cp[j] > m:
                    m = cp[j]
            cp[i] = est[i][1] + m
        eng_t = {e: 0.0 for e in self.ENGS}
        fin = [0.0] * n
        ready_at = [0.0] * n
        ready = [i for i in range(n) if indeg[i] == 0]
        order = []
        LAT_X, LAT_S = SCHED_P[0], SCHED_P[1]
        cur_tab = [None]
        TAB_SWITCH = 1300.0
        WINDOW = SCHED_P[2]
        while ready:
            best, bkey = None, None
            lo = min(ready)
            for i in ready:
                if i > lo + WINDOW:
                    continue
                st = max(eng_t[ops[i].eng], ready_at[i])
                tb = ops[i].tab
                if tb is not None and tb != cur_tab[0]:
                    st += TAB_SWITCH
                key = (st, -cp[i], i)
                if bkey is None or key < bkey:
                    best, bkey = i, key
            i = best
            ready.remove(i)
            st = bkey[0]
            if ops[i].tab is not None:
                cur_tab[0] = ops[i].tab
            eng_t[ops[i].eng] = st + est[i][0]
            fin[i] = st + est[i][1]
            order.append(i)
            for j in succs[i]:
                lat = LAT_S if ops[j].eng == ops[i].eng else LAT_X
                if fin[i] + lat > ready_at[j]:
                    ready_at[j] = fin[i] + lat
                indeg[j] -= 1
                if indeg[j] == 0:
                    ready.append(j)
        assert len(order) == n
        old = [ops[i] for i in order]
        self.ops = []
        self.last_w = {}
        self.readers = {}
        for o in old:
            self.add(o.eng, o.fn, o.reads, o.writes, o.dma)
        self.sim_time = max(fin) if fin else 0.0

    def finalize(self):
        ops = self.ops
        E = self.ENGS
        seqc = {e: 0 for e in E}
        clock = {e: {x: 0 for x in E} for e in E}
        for op in ops:
            seqc[op.eng] += 1
            op.seq = seqc[op.eng]
            raw, other = op.waits
            ck = clock[op.eng]
            need = []
            for i in sorted(raw | other, key=lambda i: -ops[i].seq):
                p = ops[i]
                if p.dma:
                    need.append(i)
                    pc = p.clock
                    for e2 in E:
                        if pc[e2] > ck[e2]:
                            ck[e2] = pc[e2]
                    continue
                if p.eng == op.eng and not op.dma:
                    if p.eng == "pe" or i not in raw:
                        continue
                if ck[p.eng] >= p.seq:
                    continue
                need.append(i)
                pc = p.clock
                for e2 in E:
                    if pc[e2] > ck[e2]:
                        ck[e2] = pc[e2]
            for i in need:
                ops[i].signal = True
            op.waits = need
            snap = dict(ck)
            if not op.dma:
                snap[op.eng] = max(snap[op.eng], 0)
                snap = dict(snap)
                snap[op.eng] = op.seq
            op.clock = snap

    def emit(self, sched=True):
        nc = self.nc
        if sched and SCHED[0]:
            self.schedule()
        self.finalize()
        ops = self.ops
        pool = SemPool.get(nc)
        for op in ops:
            if op.signal and not op.dma:
                c = pool.sigc[op.eng]
                pool.sigc[op.eng] += 1
                op.sig = (op.eng, c // SEM_CH, c % SEM_CH + 1)
                pool.eng_sem(op.eng, c // SEM_CH)
        import contextlib
        with contextlib.ExitStack() as st:
            esems = pool.esems
            dsems = pool.dsems
            dcount = pool.dcount
            drr = pool.drr
            for op in ops:
                if op.dma:
                    q = op.eng
                    j = drr[q]
                    drr[q] = (j + 1) % N_DMA_SEMS[q]
                    prev = dcount[q][j]
                    dcount[q][j] += 16
                    op.sig = ("dma", q, j, dcount[q][j], prev)
            block = st.enter_context(nc.Block())
            per_eng = {e: [op for op in ops if op.eng == e] for e in self.ENGS}
            self.stats = {e: [len(per_eng[e]), 0] for e in self.ENGS}

            def run(eng_name, eng):
                waited = {}
                nw = 0
                for op in per_eng[eng_name]:
                    if op.dma:
                        _, q, j, val, prev = op.sig
                        if prev > 0 and waited.get(("d", q, j), 0) < prev:
                            eng.wait_ge(dsems[q][j], prev)
                            waited[("d", q, j)] = prev
                            nw += 1
                    for i in op.waits:
                        p = ops[i]
                        if p.dma:
                            _, q, j, val, _ = p.sig
                            key = ("d", q, j)
                            sem = dsems[q][j]
                        else:
                            e2, sj, val = p.sig
                            key = ("e", e2, sj)
                            sem = esems[e2][sj]
                        if waited.get(key, 0) >= val:
                            continue
                        eng.wait_ge(sem, val)
                        waited[key] = val
                        nw += 1
                    ins = op.fn(eng)
                    if op.dma:
                        _, q, j, val, _ = op.sig
                        ins.then_inc(dsems[q][j], 16)
                    elif op.signal:
                        e2, sj, val = op.sig
                        ins.then_inc(esems[e2][sj], 1)
                if eng_name in N_DMA_SEMS:
                    for j, c in enumerate(dcount[eng_name]):
                        if c > 0:
                            eng.wait_ge(dsems[eng_name][j], c)
                self.stats[eng_name][1] = nw

            @block.tensor
            def _(e):
                run("pe", e)

            @block.scalar
            def _(e):
                run("act", e)

            @block.vector
            def _(e):
                run("dve", e)

            @block.gpsimd
            def _(e):
                run("pool", e)

            @block.sync
            def _(e):
                run("sp", e)


import contextlib
import ml_dtypes
from concourse.bass_utils import run_bass_kernel_spmd

T = 4096
DM = 1024
DFF = 4096
NB = T // 128
EPS = 1e-6
NCORES = 8


class Ctx:
    CNT = [0]

    def __init__(self, nc):
        self.nc = nc
        self.st = contextlib.ExitStack()

    def sb(self, shape, dt, name=None):
        Ctx.CNT[0] += 1
        return self.st.enter_context(self.nc.sbuf_tensor(name or f"sb{Ctx.CNT[0]}", list(shape), dt))

    def ps(self, shape, dt, name=None):
        Ctx.CNT[0] += 1
        return self.st.enter_context(self.nc.psum_tensor(name or f"ps{Ctx.CNT[0]}", list(shape), dt))

    def close(self):
        self.st.close()


def rmsnorm_block(P, x_ap, xkey, nw, ident, hb, hbkey, pT, pTkey, hT_dst, hTkey, tmp, cp_eng):
    jk, ssq, std = tmp["jk"], tmp["ssq"], tmp["std"]
    k = tmp["key"]
    P.act(lambda e: e.activation(out=jk[:], in_=x_ap, func=AF.Square, accum_out=ssq[:]),
          reads=[xkey], writes=[("jk", k), ("ssq", k)])
    P.act(lambda e: e.activation(out=std[:], in_=ssq[:], func=AF.Ln, scale=1.0 / DM, bias=EPS),
          reads=[("ssq", k)], writes=[("std", k)])
    P.act(lambda e: e.activation(out=std[:], in_=std[:], func=AF.Exp, scale=-0.5), reads=[("std", k)], writes=[("std", k)])
    P.dve(lambda e: e.scalar_tensor_tensor(out=hb[:], in0=x_ap, scalar=std[:, 0:1], in1=nw[:],
                                           op0=ALU.mult, op1=ALU.mult),
          reads=[xkey, ("std", k), "nw"], writes=[hbkey])
    for c in range(8):
        P.pe(lambda e, c=c: e.transpose(pT[:, c * 128:(c + 1) * 128], hb[:, c * 128:(c + 1) * 128], ident[:]),
             reads=[hbkey, "ident"], writes=[pTkey])
    src = pT[:].rearrange("p (c t) -> p c t", c=8)
    if cp_eng == "act":
        P.act(lambda e: e.copy(hT_dst, src), reads=[pTkey], writes=[hTkey])
    else:
        P.dve(lambda e: e.tensor_copy(hT_dst, src), reads=[pTkey], writes=[hTkey])


def load_consts(P, C, dram):
    identf = C.sb([128, 128], F32)
    ident = C.sb([128, 128], BF16)
    P.dma("sp", identf[:], dram["c_ident"].ap(), writes=["identf"])
    P.dve(lambda e: e.tensor_copy(ident[:], identf[:]), reads=["identf"], writes=["ident"])
    return ident


def phase_mlp(nc, dram, src, dst, layer, pre=None):
    C = Ctx(nc)
    P = Prog(nc)
    TT = 256
    NT = T // TT
    w1 = dram["mlp_w1"].ap()[layer]
    w2 = dram["mlp_w2"].ap()[layer]
    W1s = C.sb([128, 8, DFF], BF16)
    W2s = C.sb([128, 32, DM], BF16)
    nw = C.sb([128, DM], F32)
    ident = load_consts(P, C, dram)
    P.dma("sp", nw[:], dram["mlp_norm_w"].ap()[layer].partition_broadcast(128), writes=["nw"])
    if pre is not None:
        aoT, wo_ap = pre
        Wo = C.sb([128, 8, DM], BF16)
        at = [C.sb([128, 8, TT], BF16) for _ in range(2)]
        for i in range(2):
            P.dma("pool", Wo[:, :, i * 512:(i + 1) * 512],
                  wo_ap.rearrange("(c p) n -> p c n", p=128)[:, :, i * 512:(i + 1) * 512], writes=[("Wo", i)])
    for i in range(8):
        P.dma("pool", W1s[:, :, i * 512:(i + 1) * 512],
              w1.rearrange("(c p) n -> p c n", p=128)[:, :, i * 512:(i + 1) * 512], writes=[("W1", i)])
    for i in range(8):
        P.dma("pool", W2s[:, i * 4:(i + 1) * 4, :],
              w2.rearrange("(c p) n -> p c n", p=128)[:, i * 4:(i + 1) * 4, :], writes=[("W2", i)])
    xt = [C.sb([128, 2, DM], F32) for _ in range(2)]
    hb = [C.sb([128, DM], BF16) for _ in range(2)]
    hT = [C.sb([128, 8, TT], BF16) for _ in range(2)]
    rl = [C.sb([128, TT], F32) for _ in range(2)]
    hid = C.sb([128, 32, TT], BF16)
    tmps = [dict(jk=C.sb([128, DM], BF16), ssq=C.sb([128, 1], F32), std=C.sb([128, 1], F32), key=i) for i in range(2)]
    pT = [C.ps([128, DM], BF16) for _ in range(2)]
    pH = [C.ps([128, 512], F32) for _ in range(2)]
    pY = [C.ps([128, 512], F32) for _ in range(4)]
    srca = src.ap()
    dsta = dst.ap()

    def load(t):
        s = t % 2
        P.dma("sp", xt[s][:], srca[t * TT:(t + 1) * TT, :].rearrange("(b p) d -> p b d", p=128),
              writes=[("xt", s, 0), ("xt", s, 1)])
        if pre is not None:
            P.dma("sp", at[s][:], aoT.ap()[:, t * TT:(t + 1) * TT].rearrange("(c p) t -> p c t", p=128),
                  writes=[("at", s)])

    def norm(t):
        s = t % 2
        for b in range(2):
            if pre is not None:
                for half in range(2):
                    bank = pY[b * 2 + half]
                    for c in range(8):
                        P.pe(lambda e, c=c, bank=bank, b=b, half=half: e.matmul(
                            bank[:], at[s][:, c, b * 128:(b + 1) * 128], Wo[:, c, half * 512:(half + 1) * 512],
                            start=(c == 0), stop=(c == 7)),
                            reads=[("at", s), ("Wo", half)], writes=[("pY", b * 2 + half)])
                    xs = xt[s][:, b, half * 512:(half + 1) * 512]
                    P.dve(lambda e, xs=xs, bank=bank: e.tensor_tensor(out=xs, in0=bank[:], in1=xs, op=ALU.add),
                          reads=[("pY", b * 2 + half), ("xt", s, b)], writes=[("xt", s, b)])
            rmsnorm_block(P, xt[s][:, b, :], ("xt", s, b), nw, ident, hb[b], ("hb", b), pT[b], ("pT", b),
                          hT[s][:, :, b * 128:(b + 1) * 128], ("hT", s, b), tmps[b], "act" if b == 0 else "dve")

    def stage1(t):
        s = t % 2
        for f in range(32):
            bank = pH[f % 2]
            for c in range(8):
                P.pe(lambda e, c=c, f=f, bank=bank: e.matmul(
                    bank[:, 0:TT], W1s[:, c, f * 128:(f + 1) * 128], hT[s][:, c, :],
                    start=(c == 0), stop=(c == 7)),
                    reads=[("W1", f // 4), ("hT", s, 0), ("hT", s, 1)], writes=[("pH", f % 2)])
            r = rl[f % 2]
            P.act(lambda e, r=r, bank=bank: e.activation(out=r[:], in_=bank[:, 0:TT], func=AF.Relu),
                  reads=[("pH", f % 2)], writes=[("rl", f % 2)])
            P.pool(lambda e, r=r, f=f: e.tensor_tensor(out=hid[:, f, :], in0=r[:], in1=r[:], op=ALU.mult),
                   reads=[("rl", f % 2)], writes=[("hid", f)])

    def stage2(t):
        s = t % 2
        for b in range(2):
            for half in range(2):
                bank = pY[b * 2 + half]
                for f in range(32):
                    P.pe(lambda e, f=f, bank=bank, b=b, half=half: e.matmul(
                        bank[:], hid[:, f, b * 128:(b + 1) * 128], W2s[:, f, half * 512:(half + 1) * 512],
                        start=(f == 0), stop=(f == 31)),
                        reads=[("hid", f), ("W2", f // 4)], writes=[("pY", b * 2 + half)])
                xs = xt[s][:, b, half * 512:(half + 1) * 512]
                P.dve(lambda e, xs=xs, bank=bank: e.tensor_tensor(out=xs, in0=bank[:], in1=xs, op=ALU.add),
                      reads=[("pY", b * 2 + half), ("xt", s, b)], writes=[("xt", s, b)])
        P.dma("sp", dsta[t * TT:(t + 1) * TT, :].rearrange("(b p) d -> p b d", p=128), xt[s][:],
              reads=[("xt", s, 0), ("xt", s, 1)], writes=[("dst", t)])

    load(0)
    load(1)
    norm(0)
    for t in range(NT):
        stage1(t)
        if t + 1 < NT:
            norm(t + 1)
        stage2(t)
        if t + 2 < NT:
            load(t + 2)
    P.emit()
    C.close()
    return P.stats


def host_consts():
    c = {}
    c["c_ident"] = np.eye(128, dtype=np.float32)
    return c


IN_SHAPES = {
    "x": [T, DM],
    "ev_norm_w": [1, DM], "ev_w_in": [1, DM, 4112], "ev_conv_wT": [128, 12, 4], "ev_conv_bT": [128, 12],
    "ev_dt_bias": [1, 16], "ev_a_log": [1, 16], "ev_d_skip": [1, 16], "ev_ssd_norm_w": [1, 1024],
    "ev_q_norm": [1, 64], "ev_k_norm": [1, 64], "ev_sinks": [1, 16], "ev_w_out": [1, 2048, DM],
    "od_norm_w": [1, DM], "od_w_in": [1, DM, 3072], "od_q_norm": [1, 64], "od_k_norm": [1, 64],
    "od_lam_q1": [1, 64], "od_lam_k1": [1, 64], "od_lam_q2": [1, 64], "od_lam_k2": [1, 64],
    "od_sub_norm": [1, 128], "od_w_out": [1, DM, DM],
    "mlp_norm_w": [2, DM], "mlp_w1": [2, DM, DFF], "mlp_w2": [2, DFF, DM],
}


def declare(nc, consts, only=None):
    dram = {}
    for k, shp in IN_SHAPES.items():
        if only is not None and k not in only:
            continue
        dram[k] = nc.dram_tensor(k, shp, F32, kind="ExternalInput")
    for k, v in consts.items():
        dt = BF16 if v.dtype == ml_dtypes.bfloat16 else F32
        dram[k] = nc.dram_tensor(k, list(v.shape), dt, kind="ExternalInput")
    return dram


def load_col(P, C, row_ap, n, reps, key, scale=None):
    col = C.sb([n * reps, 1], F32)
    for r in range(reps):
        P.dma("sp", col[r * n:(r + 1) * n, :], row_ap.rearrange("(d o) -> d o", o=1), writes=[(key, "raw", r)])
    if scale is not None:
        col2 = C.sb([n * reps, 1], F32)
        P.dve(lambda e: e.tensor_scalar(col2[:], col[:], float(scale), None, op0=ALU.mult),
              reads=[(key, "raw", r) for r in range(reps)], writes=[key])
        return col2
    P.dve(lambda e: e.tensor_copy(col[:], col[:]), reads=[(key, "raw", r) for r in range(reps)], writes=[key])
    return col


def phase_qkv1(nc, dram, src, qTs, kTs, vs):
    C = Ctx(nc)
    P = Prog(nc)
    TT = 512
    NT = T // TT
    win = dram["od_w_in"].ap()[0]
    Win = C.sb([128, 8, 3072], BF16)
    nw = C.sb([128, DM], F32)
    ident = load_consts(P, C, dram)
    onesbd = C.sb([128, 128], BF16)
    P.dma("sp", onesbd[:], dram["c_onesbd"].ap(), writes=["onesbd"])
    P.dma("sp", nw[:], dram["od_norm_w"].ap()[0].partition_broadcast(128), writes=["nw"])
    wq = load_col(P, C, dram["od_q_norm"].ap()[0], 64, 2, "wq", scale=0.125)
    wk = load_col(P, C, dram["od_k_norm"].ap()[0], 64, 2, "wk")
    for i in range(6):
        P.dma("pool", Win[:, :, i * 512:(i + 1) * 512],
              win.rearrange("(c p) n -> p c n", p=128)[:, :, i * 512:(i + 1) * 512], writes=[("Win", i)])
    xt = [C.sb([128, 4, DM], F32) for _ in range(2)]
    hb = [C.sb([128, DM], BF16) for _ in range(2)]
    hT = [C.sb([128, 8, TT], BF16) for _ in range(2)]
    tmps = [dict(jk=C.sb([128, DM], BF16), ssq=C.sb([128, 1], F32), std=C.sb([128, 1], F32), key=i) for i in range(2)]
    sq = [C.sb([128, TT], BF16) for _ in range(2)]
    sd = [C.sb([128, TT], F32) for _ in range(2)]
    qo = [C.sb([128, TT], BF16) for _ in range(4)]
    qs = [C.sb([128, TT], F32) for _ in range(3)]
    vo = [C.sb([128, DM], BF16) for _ in range(2)]
    pT = [C.ps([128, DM], BF16) for _ in range(2)]
    pQ = [C.ps([128, 512], F32) for _ in range(2)]
    pS = [C.ps([128, 512], F32) for _ in range(2)]
    pV = [C.ps([128, 512], F32) for _ in range(2)]
    srca = src.ap()

    def load(t):
        s = t % 2
        P.dma("sp", xt[s][:], srca[t * TT:(t + 1) * TT, :].rearrange("(b p) d -> p b d", p=128),
              writes=[("xt", s, b) for b in range(4)])

    def norm(t):
        s = t % 2
        for b in range(4):
            rmsnorm_block(P, xt[s][:, b, :], ("xt", s, b), nw, ident, hb[b % 2], ("hb", b % 2), pT[b % 2],
                          ("pT", b % 2), hT[s][:, :, b * 128:(b + 1) * 128], ("hT", s, b), tmps[b % 2],
                          "act" if b % 2 == 0 else "dve")

    cnt = [0]

    def proj(t):
        s = t % 2
        hkeys = [("hT", s, b) for b in range(4)]
        for ch in range(16):
            i = cnt[0] % 2
            o = cnt[0] % 4
            cnt[0] += 1
            for c in range(8):
                P.pe(lambda e, c=c, ch=ch, i=i: e.matmul(pQ[i][:], Win[:, c, ch * 128:(ch + 1) * 128], hT[s][:, c, :],
                                                        start=(c == 0), stop=(c == 7)),
                     reads=[("Win", ch // 4)] + hkeys, writes=[("pQ", i)])
            k3 = (cnt[0] - 1) % 3
            P.dve(lambda e, i=i, k3=k3: e.tensor_copy(qs[k3][:], pQ[i][:]), reads=[("pQ", i)], writes=[("qs", k3)])
            P.act(lambda e, i=i, k3=k3: e.activation(out=sq[i][:], in_=qs[k3][:], func=AF.Square),
                  reads=[("qs", k3)], writes=[("sq", i)])
            P.pe(lambda e, i=i: e.matmul(pS[i][:], onesbd[:], sq[i][:], start=True, stop=True),
                 reads=["onesbd", ("sq", i)], writes=[("pS", i)])
            P.act(lambda e, i=i: e.activation(out=sd[i][:], in_=pS[i][:], func=AF.Ln, scale=1.0 / 64, bias=EPS),
                  reads=[("pS", i)], writes=[("sd", i)])
            P.act(lambda e, i=i: e.activation(out=sd[i][:], in_=sd[i][:], func=AF.Exp, scale=-0.5),
                  reads=[("sd", i)], writes=[("sd", i)])
            wcol = wq if ch < 8 else wk
            P.dve(lambda e, i=i, o=o, wcol=wcol, k3=k3: e.scalar_tensor_tensor(
                out=qo[o][:], in0=qs[k3][:], scalar=wcol[:, 0:1], in1=sd[i][:], op0=ALU.mult, op1=ALU.mult),
                reads=[("qs", k3), ("sd", i), "wq", "wk"], writes=[("qo", o)])
            dstT = qTs if ch < 8 else kTs
            P.dma("sp", dstT.ap()[ch % 8, :, t * TT:(t + 1) * TT], qo[o][:], reads=[("qo", o)],
                  writes=[("qk", ch, t)])
        for b in range(4):
            for half in range(2):
                for c in range(8):
                    P.pe(lambda e, c=c, b=b, half=half: e.matmul(
                        pV[half][:], hT[s][:, c, b * 128:(b + 1) * 128],
                        Win[:, c, 2048 + half * 512:2048 + (half + 1) * 512], start=(c == 0), stop=(c == 7)),
                        reads=[("Win", 4 + half), ("hT", s, b)], writes=[("pV", half)])
                dst = vo[b % 2][:, half * 512:(half + 1) * 512]
                if half == 0:
                    P.act(lambda e, dst=dst: e.copy(dst, pV[0][:]), reads=[("pV", 0)], writes=[("vo", b % 2, 0)])
                else:
                    P.dve(lambda e, dst=dst: e.tensor_copy(dst, pV[1][:]), reads=[("pV", 1)], writes=[("vo", b % 2, 1)])
            r0 = t * TT + b * 128
            P.dma("sp", vs.ap()[r0:r0 + 128, :], vo[b % 2][:], reads=[("vo", b % 2, 0), ("vo", b % 2, 1)],
                  writes=[("vs", t, b)])

    load(0)
    load(1)
    norm(0)
    for t in range(NT):
        if t + 1 < NT:
            norm(t + 1)
        proj(t)
        if t + 2 < NT:
            load(t + 2)
    P.emit()
    C.close()
    return P.stats


LAM_INIT = 0.8 - 0.6 * float(np.exp(-0.3 * 1))
NEG = -30000.0


def consts_diff():
    c = {}
    bd = np.zeros((8, 128, 640), np.float32)
    ct = np.zeros((128, 8, 32), np.float32)
    kr = np.arange(128)[:, None]
    u = np.arange(640)[None, :]
    for h in range(8):
        slope = 2.0 ** (-(h + 1))
        b = -slope * (u - kr).astype(np.float32)
        diag = np.where((kr // 64) <= (u // 64), -slope * np.abs(u - kr), NEG)
        b = np.where(u < 128, diag, b)
        bd[h] = b
        for n in range(1, 32):
            ct[:, h, n] = -slope * 128.0 * (n - 1)
    c["c_bd"] = np.ascontiguousarray(bd.transpose(1, 0, 2))
    pos = np.arange(T)
    pr_, pb_ = (pos % 128).astype(np.float32), (pos // 128).astype(np.float32)
    posQ = np.stack([-pr_, -128.0 * pb_, np.ones(T, np.float32), np.ones(T, np.float32)], axis=0)
    posK = np.zeros((8, 4, T), np.float32)
    for h in range(8):
        slope = 2.0 ** (-(h + 1))
        posK[h] = slope * np.stack([np.ones(T, np.float32), np.ones(T, np.float32), pr_, 128.0 * pb_], axis=0)
    assert np.array_equal(posQ.astype(ml_dtypes.bfloat16).astype(np.float32), posQ)
    assert np.array_equal(posK.astype(ml_dtypes.bfloat16).astype(np.float32), posK)
    c["c_posQ"] = posQ.astype(ml_dtypes.bfloat16)
    c["c_posK"] = posK.astype(ml_dtypes.bfloat16)
    c["c_ones"] = np.ones((128, 128), ml_dtypes.bfloat16)
    ob = np.zeros((128, 128), np.float32)
    ob[:64, :64] = 1
    ob[64:, 64:] = 1
    c["c_onesbd"] = ob.astype(ml_dtypes.bfloat16)
    return c


def phase_diffattn(nc, dram, qTs, kTs, vs, aoT):
    C = Ctx(nc)
    P = Prog(nc)
    bd = C.sb([128, 8, 640], F32)
    ones = C.sb([128, 128], BF16)
    P.dma("sp", bd[:], dram["c_bd"].ap(), writes=["bd"])
    P.dma("sp", ones[:], dram["c_ones"].ap(), writes=["ones"])
    lv = {}
    for nm in ("od_lam_q1", "od_lam_k1", "od_lam_q2", "od_lam_k2"):
        lv[nm] = C.sb([128, 64], F32)
        P.dma("sp", lv[nm][:], dram[nm].ap()[0].partition_broadcast(128), writes=[nm])
    pr = [C.sb([128, 64], F32) for _ in range(2)]
    sm = [C.sb([128, 1], F32) for _ in range(2)]
    nlam = C.sb([128, 1], F32)
    for i, (a, b) in enumerate((("od_lam_q1", "od_lam_k1"), ("od_lam_q2", "od_lam_k2"))):
        P.dve(lambda e, i=i, a=a, b=b: e.tensor_tensor(out=pr[i][:], in0=lv[a][:], in1=lv[b][:], op=ALU.mult),
              reads=[a, b], writes=[("pr", i)])
        P.dve(lambda e, i=i: e.reduce_sum(sm[i][:], pr[i][:], axis=AX.X), reads=[("pr", i)], writes=[("sm", i)])
        P.act(lambda e, i=i: e.activation(out=sm[i][:], in_=sm[i][:], func=AF.Exp), reads=[("sm", i)], writes=[("sm", i)])
    P.dve(lambda e: e.tensor_tensor(out=nlam[:], in0=sm[1][:], in1=sm[0][:], op=ALU.subtract),
          reads=[("sm", 0), ("sm", 1)], writes=["nlam"])
    P.dve(lambda e: e.tensor_scalar(nlam[:], nlam[:], -LAM_INIT, None, op0=ALU.add), reads=["nlam"], writes=["nlam"])
    wsub = load_col(P, C, dram["od_sub_norm"].ap()[0], 128, 1, "wsub", scale=1.0 - LAM_INIT)

    qT = [[C.sb([68, T], BF16) for _ in range(2)] for _ in range(2)]
    kT = [[C.sb([68, T], BF16) for _ in range(2)] for _ in range(2)]
    for s_ in range(2):
        for m_ in range(2):
            P.dma("sp", qT[s_][m_][64:68, :], dram["c_posQ"].ap(), writes=[("qpos", s_, m_)])
    V = [C.sb([128, NB, 128], BF16) for _ in range(2)]
    tt = [C.sb([128, 2, 512], F32) for _ in range(8)]
    Pm = [C.sb([128, 2, 512], BF16) for _ in range(8)]
    rd = [C.sb([128, 512], F32) for _ in range(2)]
    o12 = [C.sb([128, 512], F32) for _ in range(2)]
    oo = C.sb([128, 512], F32)
    sq = C.sb([128, 512], BF16)
    sd = C.sb([128, 512], F32)
    ob = [C.sb([128, 512], BF16) for _ in range(2)]
    pS = [C.ps([128, 2, 512], F32) for _ in range(2)]
    pO = [C.ps([128, 512], F32) for _ in range(2)]
    pD = [C.ps([128, 512], F32) for _ in range(2)]

    def load_head(h):
        s = h % 2
        for m in range(2):
            P.dma("sp", qT[s][m][0:64, :], qTs.ap()[h, m * 64:(m + 1) * 64, :], writes=[("qT", s, m)])
            P.dma("sp", kT[s][m][0:64, :], kTs.ap()[h, m * 64:(m + 1) * 64, :], writes=[("kT", s, m)])
            P.dma("sp", kT[s][m][64:68, :], dram["c_posK"].ap()[h], writes=[("kpos", s, m)])
        P.dma("sp", V[s][:], vs.ap()[:, h * 128:(h + 1) * 128].rearrange("(j p) d -> p j d", p=128),
              writes=[("V", s)])

    steps = [(h, Q, j) for h in range(8) for Q in range(8) for j in range(4 * Q + 4)]

    def geom(i):
        h, Q, j = steps[i]
        jj = j - 4 * Q
        c0 = jj * 128 if jj > 0 else 0
        return h, Q, j, c0

    def S(i):
        h, Q, j, c0 = geom(i)
        s = h % 2
        full = j < 4 * Q
        kk = 68 if full else 64
        for m in range(2):
            P.pe(lambda e, m=m: e.matmul(pS[i % 2][:, m, c0:512], kT[s][m][0:kk, j * 128:(j + 1) * 128],
                                         qT[s][m][0:kk, Q * 512 + c0:(Q + 1) * 512], start=True, stop=True),
                 reads=[("kT", s, m), ("qT", s, m), ("kpos", s, m), ("qpos", s, m)], writes=[("pS", i % 2)])

    def soft(i):
        h, Q, j, c0 = geom(i)
        full = j < 4 * Q
        N = 512 - c0
        if full:
            P.act(lambda e: e.activation(out=Pm[i % 8][:], in_=pS[i % 2][:], func=AF.Exp),
                  reads=[("pS", i % 2)], writes=[("P", i % 8)])
        else:
            P.dve(lambda e: e.tensor_tensor(out=tt[i % 8][:, :, c0:512], in0=pS[i % 2][:, :, c0:512],
                                            in1=bd[:, h, 0:N].unsqueeze(1).broadcast_to([128, 2, N]), op=ALU.add),
                  reads=[("pS", i % 2), "bd"], writes=[("t", i % 8)])
            P.act(lambda e: e.activation(out=Pm[i % 8][:, :, c0:512], in_=tt[i % 8][:, :, c0:512], func=AF.Exp),
                  reads=[("t", i % 8)], writes=[("P", i % 8)])

    def pv(i):
        h, Q, j, c0 = geom(i)
        s = h % 2
        last = 4 * Q + 3
        for m in range(2):
            P.pe(lambda e, m=m: e.matmul(pO[m][:, c0:512], V[s][:, j, :], Pm[i % 8][:, m, c0:512],
                                         start=(j == 0), stop=(j == last)),
                 reads=[("V", s), ("P", i % 8)], writes=[("pO", m)])
        for m in range(2):
            P.pe(lambda e, m=m: e.matmul(pD[m][:, c0:512], ones[:], Pm[i % 8][:, m, c0:512],
                                         start=(j == 0), stop=(j == last)),
                 reads=["ones", ("P", i % 8)], writes=[("pD", m)])

    ecnt = [0]

    def epilogue(h, Q):
        k = ecnt[0] % 2
        ecnt[0] += 1
        for m in range(2):
            P.dve(lambda e, m=m: e.tensor_copy(o12[m][:], pO[m][:]), reads=[("pO", m)], writes=[("o12", m)])
            P.act(lambda e, m=m: e.activation(out=rd[m][:], in_=pD[m][:], func=AF.Ln), reads=[("pD", m)], writes=[("rd", m)])
        for m in range(2):
            P.act(lambda e, m=m: e.activation(out=rd[m][:], in_=rd[m][:], func=AF.Exp, scale=-1.0),
                  reads=[("rd", m)], writes=[("rd", m)])
            P.dve(lambda e, m=m: e.tensor_tensor(out=o12[m][:], in0=o12[m][:], in1=rd[m][:], op=ALU.mult),
                  reads=[("o12", m), ("rd", m)], writes=[("o12", m)])
        P.dve(lambda e: e.scalar_tensor_tensor(out=oo[:], in0=o12[1][:], scalar=nlam[:, 0:1], in1=o12[0][:],
                                              op0=ALU.mult, op1=ALU.add),
              reads=[("o12", 0), ("o12", 1), "nlam"], writes=["oo"])
        P.act(lambda e: e.activation(out=sq[:], in_=oo[:], func=AF.Square), reads=["oo"], writes=["sq"])
        P.pe(lambda e: e.matmul(pD[0][:], ones[:], sq[:], start=True, stop=True),
             reads=["ones", "sq"], writes=[("pD", 0)])
        P.act(lambda e: e.activation(out=sd[:], in_=pD[0][:], func=AF.Ln, scale=1.0 / 128, bias=EPS),
              reads=[("pD", 0)], writes=["sd"])
        P.act(lambda e: e.activation(out=sd[:], in_=sd[:], func=AF.Exp, scale=-0.5), reads=["sd"], writes=["sd"])
        P.dve(lambda e: e.scalar_tensor_tensor(out=ob[k][:], in0=oo[:], scalar=wsub[:, 0:1], in1=sd[:],
                                              op0=ALU.mult, op1=ALU.mult),
              reads=["oo", "sd", "wsub"], writes=[("ob", k)])
        P.dma("sp", aoT.ap()[h * 128:(h + 1) * 128, Q * 512:(Q + 1) * 512], ob[k][:], reads=[("ob", k)],
              writes=[("aoT", h, Q)])

    load_head(0)
    load_head(1)
    S(0)
    for i in range(len(steps)):
        h, Q, j, c0 = geom(i)
        if i + 1 < len(steps):
            S(i + 1)
        soft(i)
        pv(i)
        if j == 4 * Q + 3:
            epilogue(h, Q)
            if Q == 7 and h + 2 < 8:
                load_head(h + 2)
    P.emit()
    C.close()
    return P.stats


def host_consts_all():
    c = host_consts()
    c.update(consts_diff())
    return c


def scratch(nc):
    s = {}
    s["r1"] = nc.dram_tensor("r1", [T, DM], F32, kind="Internal")
    s["r2"] = nc.dram_tensor("r2", [T, DM], F32, kind="Internal")
    s["qTs"] = nc.dram_tensor("qTs", [8, 128, T], BF16, kind="Internal")
    s["kTs"] = nc.dram_tensor("kTs", [8, 128, T], BF16, kind="Internal")
    s["vs"] = nc.dram_tensor("vs", [T, DM], BF16, kind="Internal")
    s["aoT"] = nc.dram_tensor("aoT", [DM, T], BF16, kind="Internal")
    return s


def consts_swa():
    c = {}
    kr = np.arange(128)[:, None]
    qr = np.arange(128)[None, :]
    cur = np.where((kr // 64) <= (qr // 64), np.abs(qr - kr), 60000.0)
    prev = np.where((qr // 64 == 1) & (kr // 64 == 0), 60000.0, qr + 128 - kr)
    c["c_distm"] = np.ascontiguousarray(np.stack([prev, cur], axis=1).astype(np.float32))
    sw = np.zeros((128, 128), np.float32)
    for i in range(128):
        sw[i, (i + 64) % 128] = 1.0
    c["c_swap"] = sw.astype(ml_dtypes.bfloat16)
    return c


def qk_norm_chunk(P, pQ, pQkey, sq, sqkey, pS, pSkey, sd, sdkey, onesbd, wcol, wkey, out_ap, outkey, nfeat=64,
                  qs=None, qskey=None):
    if qs is not None:
        P.dve(lambda e, src_=pQ: e.tensor_copy(qs[:], src_[:]), reads=[pQkey], writes=[qskey])
        pQ, pQkey = qs, qskey
    P.act(lambda e: e.activation(out=sq[:], in_=pQ[:], func=AF.Square), reads=[pQkey], writes=[sqkey])
    P.pe(lambda e: e.matmul(pS[:], onesbd[:], sq[:], start=True, stop=True), reads=["onesbd", sqkey], writes=[pSkey])
    P.act(lambda e: e.activation(out=sd[:], in_=pS[:], func=AF.Ln, scale=1.0 / nfeat, bias=EPS),
          reads=[pSkey], writes=[sdkey])
    P.act(lambda e: e.activation(out=sd[:], in_=sd[:], func=AF.Exp, scale=-0.5), reads=[sdkey], writes=[sdkey])
    P.dve(lambda e: e.scalar_tensor_tensor(out=out_ap, in0=pQ[:], scalar=wcol[:, 0:1], in1=sd[:],
                                           op0=ALU.mult, op1=ALU.mult),
          reads=[pQkey, sdkey, wkey], writes=[outkey])


def phase_swa(nc, dram, src, yT):
    C = Ctx(nc)
    P = Prog(nc)
    TT = 512
    NT = T // TT
    win = dram["ev_w_in"].ap()[0]
    W = C.sb([128, 8, 1536], BF16)
    nw = C.sb([128, DM], F32)
    ident = load_consts(P, C, dram)
    onesbd = C.sb([128, 128], BF16)
    ones = C.sb([128, 128], BF16)
    swapm = C.sb([128, 128], BF16)
    distm = C.sb([128, 2, 128], F32)
    esk = C.sb([128, 16], F32)
    P.dma("sp", onesbd[:], dram["c_onesbd"].ap(), writes=["onesbd"])
    P.dma("sp", ones[:], dram["c_ones"].ap(), writes=["ones"])
    P.dma("sp", swapm[:], dram["c_swap"].ap(), writes=["swapm"])
    P.dma("sp", distm[:], dram["c_distm"].ap(), writes=["distm"])
    P.dma("sp", nw[:], dram["ev_norm_w"].ap()[0].partition_broadcast(128), writes=["nw"])
    P.dma("sp", esk[:], dram["ev_sinks"].ap()[0].partition_broadcast(128), writes=["esk"])
    P.act(lambda e: e.activation(out=esk[:], in_=esk[:], func=AF.Exp), reads=["esk"], writes=["esk"])
    wq = load_col(P, C, dram["ev_q_norm"].ap()[0], 64, 2, "wq", scale=0.125)
    wk = load_col(P, C, dram["ev_k_norm"].ap()[0], 64, 2, "wk")
    for i in range(3):
        P.dma("pool", W[:, :, i * 512:(i + 1) * 512],
              win.rearrange("(c p) n -> p c n", p=128)[:, :, 2576 + i * 512:2576 + (i + 1) * 512], writes=[("W", i)])
    xa = [C.sb([128, DM], F32) for _ in range(2)]
    hb = [C.sb([128, DM], BF16) for _ in range(2)]
    hT = [C.sb([128, 8, TT], BF16) for _ in range(2)]
    tmps = [dict(jk=C.sb([128, DM], BF16), ssq=C.sb([128, 1], F32), std=C.sb([128, 1], F32), key=i) for i in range(2)]
    sq = [C.sb([128, TT], BF16) for _ in range(2)]
    sd = [C.sb([128, TT], F32) for _ in range(2)]
    qn = C.sb([128, 8, TT], BF16)
    qs3 = [C.sb([128, TT], F32) for _ in range(3)]
    kn = [C.sb([128, 2, TT], BF16) for _ in range(2)]
    knS = [C.sb([128, 2, TT], BF16) for _ in range(2)]
    Vt = [C.sb([128, 4, 65], BF16) for _ in range(2)]
    for i_ in range(2):
        P.pool(lambda e, i_=i_: e.memset(Vt[i_][:, :, 64:65], 1.0), writes=[("Vt1", i_)])
    ytok = [C.sb([128, DM], BF16) for _ in range(2)]
    rd4 = [C.sb([128, 4], F32) for _ in range(2)]
    tt = [[C.sb([128, 512], F32) for _ in range(2)] for _ in range(8)]
    Pm = [[C.sb([128, 512], BF16) for _ in range(2)] for _ in range(8)]
    ysw = [C.sb([128, 8, TT], BF16) for _ in range(2)]
    pT = C.ps([128, DM], BF16)
    F = [C.ps([128, 512], F32) for _ in range(7)]
    srca = src.ap()

    def norm(t):
        s = t % 2
        for b in range(4):
            g = 4 * t + b
            P.dma("sp", xa[g % 2][:], srca[g * 128:(g + 1) * 128, :], writes=[("xa", g % 2)])
            rmsnorm_block(P, xa[g % 2][:], ("xa", g % 2), nw, ident, hb[g % 2], ("hb", g % 2), pT, "pT",
                          hT[s][:, :, b * 128:(b + 1) * 128], ("hT", s, b), tmps[g % 2],
                          "act" if b % 2 == 0 else "dve")

    cnt = [0]

    def proj(t):
        s = t % 2
        hkeys = [("hT", s, b) for b in range(4)]
        for ch in range(10):
            i = cnt[0] % 2
            cnt[0] += 1
            for c in range(8):
                P.pe(lambda e, c=c, ch=ch, i=i: e.matmul(F[i][:], W[:, c, ch * 128:(ch + 1) * 128], hT[s][:, c, :],
                                                        start=(c == 0), stop=(c == 7)),
                     reads=[("W", (ch * 128) // 512)] + hkeys, writes=[("F", i)])
            if ch < 8:
                out_ap, outkey, wcol, wkey = qn[:, ch, :], ("qn", ch), wq, "wq"
            else:
                out_ap, outkey, wcol, wkey = kn[s][:, ch - 8, :], ("kn", s, ch - 8), wk, "wk"
            k3 = (cnt[0] - 1) % 3
            qk_norm_chunk(P, F[i], ("F", i), sq[i], ("sq", i), F[2 + i], ("F", 2 + i), sd[i], ("sd", i), onesbd,
                          wcol, wkey, out_ap, outkey, qs=qs3[k3], qskey=("qs3", k3))
        for ck in range(2):
            P.pe(lambda e, ck=ck: e.matmul(F[2 + ck][:], swapm[:], kn[s][:, ck, :], start=True, stop=True),
                 reads=["swapm", ("kn", s, ck)], writes=[("F", 2 + ck)])
            P.act(lambda e, ck=ck: e.copy(knS[s][:, ck, :], F[2 + ck][:]), reads=[("F", 2 + ck)],
                  writes=[("knS", s, ck)])

    def block(t, b):
        s = t % 2
        g = 4 * t + b
        cb = slice(b * 128, (b + 1) * 128)
        for c in range(8):
            P.pe(lambda e, c=c: e.matmul(F[6][:, 0:256], hT[s][:, c, cb], W[:, c, 1280:1536],
                                         start=(c == 0), stop=(c == 7)),
                 reads=[("W", 2), ("hT", s, b)], writes=[("F", 6)])
        P.dve(lambda e: e.tensor_copy(Vt[g % 2][:, :, 0:64], F[6][:, 0:256].rearrange("p (j d) -> p j d", j=4)),
              reads=[("F", 6)], writes=[("Vt", g % 2)])
        kts = [1] if g == 0 else [0, 1]
        c0 = 256 if g == 0 else 0

        def ksrc(kt, ck):
            if kt == 1:
                return kn[s], knS[s], cb, [("kn", s, ck), ("knS", s, ck)]
            if b > 0:
                return kn[s], knS[s], slice((b - 1) * 128, b * 128), [("kn", s, ck), ("knS", s, ck)]
            return kn[1 - s], knS[1 - s], slice(384, 512), [("kn", 1 - s, ck), ("knS", 1 - s, ck)]

        def do_jv(jv, half2):
            if True:
                ck, hf = jv // 2, jv % 2
                banks = (F[2 * (jv % 2)], F[2 * (jv % 2) + 1])
                bkeys = (("F", 2 * (jv % 2)), ("F", 2 * (jv % 2) + 1))
                for kt in kts:
                    kA, kB, kcols, kkeys = ksrc(kt, ck)
                    for r in range(4):
                        qh = r % 2
                        cq = 2 * jv + r // 2
                        ksrc_t = kA if hf == qh else kB
                        col = (kt * 2 + r // 2) * 128
                        P.pe(lambda e, qh=qh, cq=cq, ksrc_t=ksrc_t, col=col, kcols=kcols: e.matmul(
                            banks[qh][:, col:col + 128], ksrc_t[qh * 64:(qh + 1) * 64, ck, kcols],
                            qn[qh * 64:(qh + 1) * 64, cq, cb], start=True, stop=True),
                            reads=kkeys + [("qn", cq)], writes=[bkeys[qh]])
                pi = (4 * g + jv) % 8
                for qh in range(2):
                    t_ = tt[pi][qh]
                    for rr in range(2):
                        hq = 4 * jv + 2 * rr + qh
                        nslope = -(2.0 ** (-(hq + 1) / 2.0))
                        ov = t_[:].rearrange("p (k r q) -> p k r q", k=2, r=2)[:, kts[0]:2, rr, :]
                        bv = banks[qh][:].rearrange("p (k r q) -> p k r q", k=2, r=2)[:, kts[0]:2, rr, :]
                        P.dve(lambda e, ov=ov, bv=bv, nslope=nslope: e.scalar_tensor_tensor(
                            out=ov, in0=distm[:, kts[0]:2, :], scalar=float(nslope), in1=bv, op0=ALU.mult, op1=ALU.add),
                            reads=["distm", bkeys[qh]], writes=[("tt", pi, qh)])
                    P.act(lambda e, t_=t_, pi=pi, qh=qh: e.activation(out=Pm[pi][qh][:, c0:512], in_=t_[:, c0:512],
                                                                      func=AF.Exp),
                          reads=[("tt", pi, qh)], writes=[("Pm", pi, qh)])
                ob_ = F[4 + jv % 2]
                okey = ("F", 4 + jv % 2)
                for r in range(4):
                    qh = r % 2
                    for kt in kts:
                        vsrc = Vt[g % 2] if kt == 1 else Vt[(g - 1) % 2]
                        vkeys = [("Vt", g % 2), ("Vt1", g % 2)] if kt == 1 else [("Vt", (g - 1) % 2), ("Vt1", (g - 1) % 2)]
                        col = (kt * 2 + r // 2) * 128
                        P.pe(lambda e, vsrc=vsrc, qh=qh, r=r, col=col, kt=kt: e.matmul(
                            ob_[:, r * 65:(r + 1) * 65], Pm[pi][qh][:, col:col + 128], vsrc[:, jv, :],
                            start=(kt == kts[0]), stop=(kt == 1)),
                            reads=vkeys + [("Pm", pi, qh)], writes=[okey])
                o3 = ob_[:, 0:260].rearrange("p (r d) -> p r d", r=4)
                rdj = rd4[jv % 2]
                P.dve(lambda e: e.tensor_tensor(out=rdj[:], in0=o3[:, :, 64], in1=esk[:, 4 * jv:4 * jv + 4], op=ALU.add),
                      reads=[okey, "esk"], writes=[("rd4", jv % 2)])
                P.act(lambda e: e.activation(out=rdj[:], in_=rdj[:], func=AF.Ln), reads=[("rd4", jv % 2)], writes=[("rd4", jv % 2)])
                P.act(lambda e: e.activation(out=rdj[:], in_=rdj[:], func=AF.Exp, scale=-1.0), reads=[("rd4", jv % 2)],
                      writes=[("rd4", jv % 2)])
                P.dve(lambda e: e.tensor_tensor(
                    out=ytok[g % 2][:, jv * 256:(jv + 1) * 256].rearrange("p (r d) -> p r d", r=4), in0=o3[:, :, 0:64],
                    in1=rdj[:].unsqueeze(2).broadcast_to([128, 4, 64]), op=ALU.mult),
                    reads=[okey, ("rd4", jv % 2)], writes=[("ytok", g % 2, jv)])

        for half2 in range(2):
            for jv in (2 * half2, 2 * half2 + 1):
                do_jv(jv, half2)
        for ch in range(8):
            P.pe(lambda e, ch=ch: e.transpose(pT[:, ch * 128:(ch + 1) * 128], ytok[g % 2][:, ch * 128:(ch + 1) * 128], ident[:]),
                 reads=[("ytok", g % 2, ch // 2), "ident"], writes=["pT"])
        P.act(lambda e: e.copy(ysw[s][:, :, cb], pT[:].rearrange("p (c t) -> p c t", c=8)), reads=["pT"],
              writes=[("ysw", s, b)])

    def store(t):
        s = t % 2
        P.dma("sp", yT.ap()[1024:2048, t * TT:(t + 1) * TT].rearrange("(c p) t -> p c t", p=128), ysw[s][:],
              reads=[("ysw", s, b) for b in range(4)], writes=[("yT", t)])

    for t in range(NT):
        norm(t)
        proj(t)
        for b in range(4):
            block(t, b)
        store(t)
    P.emit()
    C.close()
    return P.stats


def host_layout_inputs(inp):
    d = {}
    cw = np.asarray(inp["ev_conv_w"])[0]
    d["ev_conv_wT"] = np.ascontiguousarray(cw.reshape(4, 12, 128).transpose(2, 1, 0))
    cb = np.asarray(inp["ev_conv_b"])[0]
    d["ev_conv_bT"] = np.ascontiguousarray(cb.reshape(12, 128).transpose(1, 0))
    sk = np.asarray(inp["ev_sinks"])[0]
    d["ev_sinksT"] = np.ascontiguousarray(sk.reshape(8, 2).transpose(1, 0))
    return d


IN_SHAPES["ev_sinksT"] = [2, 8]


def consts_ssd():
    c = {}
    i = np.arange(128)
    c["c_U"] = (i[:, None] <= i[None, :]).astype(np.float32)
    c["c_onesf"] = np.ones((128, 128), np.float32)
    c["c_maskneg"] = np.where(i[:, None] <= i[None, :], 0.0, NEG).astype(np.float32)
    return c


def phase_ssd(nc, dram, src, yT):
    C = Ctx(nc)
    P = Prog(nc)
    TT = 512
    NT = T // TT
    win = dram["ev_w_in"].ap()[0]
    W = C.sb([128, 8, 2576], BF16)
    nw = C.sb([128, DM], F32)
    ssdw = C.sb([128, DM], F32)
    cw = C.sb([128, 12, 4], F32)
    cbias = C.sb([128, 12], F32)
    dtb = C.sb([128, 16], F32)
    aneg = C.sb([128, 16], F32)
    dsk = C.sb([128, 16], F32)
    U = C.sb([128, 128], F32)
    onesf = C.sb([128, 128], F32)
    maskneg = C.sb([128, 128], F32)
    ident = load_consts(P, C, dram)
    P.dma("sp", nw[:], dram["ev_norm_w"].ap()[0].partition_broadcast(128), writes=["nw"])
    P.dma("sp", ssdw[:], dram["ev_ssd_norm_w"].ap()[0].partition_broadcast(128), writes=["ssdw"])
    P.dma("sp", cw[:], dram["ev_conv_wT"].ap(), writes=["cw"])
    P.dma("sp", cbias[:], dram["ev_conv_bT"].ap(), writes=["cbias"])
    P.dma("sp", dtb[:], dram["ev_dt_bias"].ap()[0].partition_broadcast(128), writes=["dtb"])
    P.dma("sp", aneg[:], dram["ev_a_log"].ap()[0].partition_broadcast(128), writes=["aneg"])
    P.dma("sp", dsk[:], dram["ev_d_skip"].ap()[0].partition_broadcast(128), writes=["dsk"])
    P.dma("sp", U[:], dram["c_U"].ap(), writes=["U"])
    P.dma("sp", onesf[:], dram["c_onesf"].ap(), writes=["onesf"])
    P.dma("sp", maskneg[:], dram["c_maskneg"].ap(), writes=["maskneg"])
    P.act(lambda e: e.activation(out=aneg[:], in_=aneg[:], func=AF.Exp), reads=["aneg"], writes=["aneg"])
    P.dve(lambda e: e.tensor_scalar(aneg[:], aneg[:], -1.0, None, op0=ALU.mult), reads=["aneg"], writes=["aneg"])
    wblocks = [(i * 512, (i + 1) * 512) for i in range(5)] + [(2560, 2576)]
    for i, (a, b) in enumerate(wblocks):
        P.dma("pool", W[:, :, a:b], win.rearrange("(c p) n -> p c n", p=128)[:, :, a:b], writes=[("W", i)])

    def wkeys(a, b):
        return [("W", i) for i, (x, y) in enumerate(wblocks) if x < b and y > a]

    xa = [C.sb([128, DM], F32) for _ in range(2)]
    hb = [C.sb([128, DM], BF16) for _ in range(2)]
    hT = [C.sb([128, 8, TT], BF16) for _ in range(2)]
    tmps = [dict(jk=C.sb([128, DM], BF16), ssq=C.sb([128, 1], F32), std=C.sb([128, 1], F32), key=i) for i in range(2)]
    xp = C.sb([128, 12, TT + 3], F32)
    xcs = C.sb([128, 12, TT], BF16)
    acc = [C.sb([128, TT], F32) for _ in range(2)]
    xs_tok_2 = [C.sb([128, DM], BF16) for _ in range(2)]
    B_tok_2 = [C.sb([128, 256], BF16) for _ in range(2)]
    zs_2 = [C.sb([128, DM], F32) for _ in range(2)]
    xdt_2 = [C.sb([128, DM], BF16) for _ in range(2)]
    xw_2 = [C.sb([128, DM], BF16) for _ in range(2)]
    D2 = C.sb([128, 8, 128], F32)
    sg_2 = [C.sb([128, 8, 128], F32) for _ in range(2)]
    M = [C.sb([128, 8, 128], BF16) for _ in range(2)]
    y_2 = [C.sb([128, DM], F32) for _ in range(2)]
    tmpf = C.sb([128, DM], F32)
    ysb_2 = [C.sb([128, DM], BF16) for _ in range(2)]
    S = C.sb([128, DM], F32)
    Sb = C.sb([128, DM], BF16)
    yTt = [C.sb([128, 8, TT], BF16) for _ in range(2)]
    sm = {k: C.sb([128, 4, 16], F32) for k in ("u", "au", "l", "dt", "adt", "acs", "ea", "dte", "etot", "w")}
    gss = C.sb([128, 2], F32)
    cbs_2 = [C.sb([128, 256], F32) for _ in range(2)]
    pT = C.ps([128, DM], BF16)
    F = [C.ps([128, 512], F32) for _ in range(7)]
    srca = src.ap()

    P.pool(lambda e: e.memset(xp[:, :, 0:3], 0.0), writes=[("xp", ch) for ch in range(12)])
    P.pool(lambda e: e.memset(S[:], 0.0), writes=["S"])
    P.pool(lambda e: e.memset(Sb[:], 0.0), writes=["Sb"])

    def norm(t):
        s = t % 2
        for b in range(4):
            g = 4 * t + b
            P.dma("sp", xa[g % 2][:], srca[g * 128:(g + 1) * 128, :], writes=[("xa", g % 2)])
            rmsnorm_block(P, xa[g % 2][:], ("xa", g % 2), nw, ident, hb[g % 2], ("hb", g % 2), pT, "pT",
                          hT[s][:, :, b * 128:(b + 1) * 128], ("hT", s, b), tmps[g % 2],
                          "act" if b % 2 == 0 else "dve")

    def proj_conv(t):
        s = t % 2
        hkeys = [("hT", s, b) for b in range(4)]

        def chunk(ch):
            i = ch % 2
            c0 = 1024 + ch * 128
            for c in range(8):
                P.pe(lambda e, c=c: e.matmul(F[i][:], W[:, c, c0:c0 + 128], hT[s][:, c, :], start=(c == 0), stop=(c == 7)),
                     reads=wkeys(c0, c0 + 128) + hkeys, writes=[("F", i)])
            if i == 0:
                P.act(lambda e: e.copy(xp[:, ch, 3:TT + 3], F[i][:]), reads=[("F", i)], writes=[("xp", ch)])
            else:
                P.dve(lambda e: e.tensor_copy(xp[:, ch, 3:TT + 3], F[i][:]), reads=[("F", i)], writes=[("xp", ch)])
            a = acc[i]
            P.act(lambda e: e.activation(out=a[:], in_=xp[:, ch, 3:TT + 3], func=AF.Identity,
                                         scale=cw[:, ch, 3:4], bias=cbias[:, ch:ch + 1]),
                  reads=[("xp", ch), "cw", "cbias"], writes=[("acc", i)])
            for k in (2, 1, 0):
                P.dve(lambda e, k=k: e.scalar_tensor_tensor(out=a[:], in0=xp[:, ch, k:TT + k], scalar=cw[:, ch, k:k + 1],
                                                            in1=a[:], op0=ALU.mult, op1=ALU.add),
                      reads=[("xp", ch), ("acc", i), "cw"], writes=[("acc", i)])
            P.act(lambda e: e.activation(out=xcs[:, ch, :], in_=a[:], func=AF.Silu), reads=[("acc", i)],
                  writes=[("xcs", ch)])
            P.act(lambda e: e.copy(xp[:, ch, 0:3], xp[:, ch, TT:TT + 3]), reads=[("xp", ch)], writes=[("xp", ch)])

        for ch in range(12):
            chunk(ch)

    def dt_tile(t):
        s = t % 2
        u, au, l_, dt, adt, acs, ea, dte, etot, w_ = (sm[k][:].rearrange("p b h -> p (b h)") for k in
                                                      ("u", "au", "l", "dt", "adt", "acs", "ea", "dte", "etot", "w"))
        smk = ["u", "au", "l", "dt", "adt", "acs", "ea", "dte", "etot", "w"]
        for b in range(4):
            cb = slice(b * 128, (b + 1) * 128)
            for c in range(8):
                P.pe(lambda e, c=c, b=b, cb=cb: e.matmul(F[2][:, b * 16:(b + 1) * 16], hT[s][:, c, cb], W[:, c, 2560:2576],
                                                         start=(c == 0), stop=(c == 7)),
                     reads=wkeys(2560, 2576) + [("hT", s, b)], writes=[("F", 2)])
        P.dve(lambda e: e.tensor_tensor(out=sm["u"][:], in0=F[2][:, 0:64].rearrange("p (b h) -> p b h", b=4),
                                        in1=dtb[:].unsqueeze(1).broadcast_to([128, 4, 16]), op=ALU.add),
              reads=[("F", 2), "dtb"], writes=["u"])
        P.dve(lambda e: e.tensor_scalar(au, u, -1.0, None, op0=ALU.mult), reads=["u"], writes=["au"])
        P.dve(lambda e: e.tensor_tensor(out=au, in0=au, in1=u, op=ALU.min), reads=["u", "au"], writes=["au"])
        P.act(lambda e: e.activation(out=l_, in_=au, func=AF.Exp), reads=["au"], writes=["l"])
        P.act(lambda e: e.activation(out=l_, in_=l_, func=AF.Ln, bias=1.0), reads=["l"], writes=["l"])
        P.dve(lambda e: e.scalar_tensor_tensor(out=dt, in0=u, scalar=0.0, in1=l_, op0=ALU.max, op1=ALU.add),
              reads=["u", "l"], writes=["dt"])
        P.dve(lambda e: e.tensor_tensor(out=sm["adt"][:], in0=sm["dt"][:],
                                        in1=aneg[:].unsqueeze(1).broadcast_to([128, 4, 16]), op=ALU.mult),
              reads=["dt", "aneg"], writes=["adt"])
        P.pe(lambda e: e.matmul(F[2][:, 64:128], U[:], adt, start=True, stop=True), reads=["U", "adt"], writes=[("F", 2)])
        P.pe(lambda e: e.matmul(F[2][:, 128:192], onesf[:], adt, start=True, stop=True), reads=["onesf", "adt"],
             writes=[("F", 2)])
        P.act(lambda e: e.activation(out=ea, in_=F[2][:, 64:128], func=AF.Exp), reads=[("F", 2)], writes=["ea"])
        P.dve(lambda e: e.tensor_copy(acs, F[2][:, 64:128]), reads=[("F", 2)], writes=["acs"])
        P.dve(lambda e: e.tensor_tensor(out=dte, in0=F[2][:, 128:192], in1=acs, op=ALU.subtract),
              reads=[("F", 2), "acs"], writes=["dte"])
        P.act(lambda e: e.activation(out=dte, in_=dte, func=AF.Exp), reads=["dte"], writes=["dte"])
        P.act(lambda e: e.activation(out=etot, in_=F[2][:, 128:192], func=AF.Exp), reads=[("F", 2)], writes=["etot"])
        P.dve(lambda e: e.tensor_tensor(out=w_, in0=dt, in1=dte, op=ALU.mult), reads=["dt", "dte"], writes=["w"])

    def block(t, b):
        s = t % 2
        g = 4 * t + b
        cb = slice(b * 128, (b + 1) * 128)
        hk = [("hT", s, b)]
        gp = g % 2
        xs_tok, B_tok, zs, xdt, xw, sg, y, ysb = (xs_tok_2[gp], B_tok_2[gp], zs_2[gp], xdt_2[gp], xw_2[gp],
                                                     sg_2[gp], y_2[gp], ysb_2[gp])
        for ch in range(8):
            P.pe(lambda e, ch=ch: e.transpose(pT[:, ch * 128:(ch + 1) * 128], xcs[:, ch, cb], ident[:]),
                 reads=[("xcs", ch), "ident"], writes=["pT"])
        P.act(lambda e: e.copy(xs_tok[:], pT[:]), reads=["pT"], writes=[("xs_tok", gp)])
        for ch in range(2):
            P.pe(lambda e, ch=ch: e.transpose(pT[:, ch * 128:(ch + 1) * 128], xcs[:, 8 + ch, cb], ident[:]),
                 reads=[("xcs", 8 + ch), "ident"], writes=["pT"])
        P.dve(lambda e: e.tensor_copy(B_tok[:], pT[:, 0:256]), reads=["pT"], writes=[("B_tok", gp)])
        for half in range(2):
            for c in range(8):
                P.pe(lambda e, c=c, half=half: e.matmul(F[half][:], hT[s][:, c, cb], W[:, c, half * 512:(half + 1) * 512],
                                                        start=(c == 0), stop=(c == 7)),
                     reads=wkeys(half * 512, half * 512 + 512) + hk, writes=[("F", half)])
            P.act(lambda e, half=half: e.activation(out=zs[:, half * 512:(half + 1) * 512], in_=F[half][:], func=AF.Silu),
                  reads=[("F", half)], writes=[("zs", gp, half)])
        u, au, l_, dt, adt, acs, ea, dte, etot, w_ = (sm[k][:, b, :] for k in ("u", "au", "l", "dt", "adt", "acs", "ea", "dte", "etot", "w"))
        xs3 = xs_tok[:].rearrange("p (h d) -> p h d", h=16)
        P.dve(lambda e: e.tensor_tensor(out=xdt[:].rearrange("p (h d) -> p h d", h=16), in0=xs3,
                                        in1=dt[:].unsqueeze(2).broadcast_to([128, 16, 64]), op=ALU.mult),
              reads=[("xs_tok", gp), "dt"], writes=[("xdt", gp)])
        P.pool(lambda e: e.tensor_tensor(out=xw[:].rearrange("p (h d) -> p h d", h=16), in0=xs3,
                                         in1=w_[:].unsqueeze(2).broadcast_to([128, 16, 64]), op=ALU.mult),
               reads=[("xs_tok", gp), "w"], writes=[("xw", gp)])
        for gq in range(2):
            P.pe(lambda e, gq=gq: e.matmul(F[2][:, 256 + gq * 128:256 + (gq + 1) * 128], xcs[:, 8 + gq, cb],
                                           xcs[:, 10 + gq, cb], start=True, stop=True),
                 reads=[("xcs", 8 + gq), ("xcs", 10 + gq)], writes=[("F", 2)])
        cbs = cbs_2[gp]
        P.dve(lambda e: e.tensor_copy(cbs[:], F[2][:, 256:512]), reads=[("F", 2)], writes=[("cbs", gp)])
        Fy = (F[6], F[3])

        def half_fn(hh):
            P.pool(lambda e: e.tensor_tensor(out=D2[:], in0=adt[:, 8 * hh:8 * hh + 8].unsqueeze(2).broadcast_to([128, 8, 128]),
                                             in1=U[:].unsqueeze(1).broadcast_to([128, 8, 128]), op=ALU.mult),
                   reads=["adt", "U"], writes=["D2"])
            for q4 in range(2):
                P.pe(lambda e, q4=q4: e.matmul(F[4 + q4][:], onesf[:],
                                               D2[:, 4 * q4:4 * q4 + 4, :].rearrange("p h l -> p (h l)"),
                                               start=True, stop=True),
                     reads=["onesf", "D2"], writes=[("F", 4 + q4)])
                P.dve(lambda e, q4=q4: e.tensor_tensor(
                    out=sg[:, 4 * q4:4 * q4 + 4, :], in0=F[4 + q4][:].rearrange("p (h l) -> p h l", h=4),
                    in1=acs[:, 8 * hh + 4 * q4:8 * hh + 4 * q4 + 4].unsqueeze(2).broadcast_to([128, 4, 128]),
                    op=ALU.subtract),
                    reads=[("F", 4 + q4), "acs"], writes=[("sg", gp)])
            P.dve(lambda e: e.scalar_tensor_tensor(out=sg[:], in0=sg[:], scalar=0.0,
                                                   in1=maskneg[:].unsqueeze(1).broadcast_to([128, 8, 128]),
                                                   op0=ALU.min, op1=ALU.add),
                  reads=[("sg", gp), "maskneg"], writes=[("sg", gp)])
            P.act(lambda e: e.activation(out=sg[:], in_=sg[:], func=AF.Exp), reads=[("sg", gp)], writes=[("sg", gp)])
            P.dve(lambda e: e.tensor_tensor(
                out=M[hh][:], in0=sg[:],
                in1=cbs[:, hh * 128:(hh + 1) * 128].unsqueeze(1).broadcast_to([128, 8, 128]), op=ALU.mult),
                reads=[("sg", gp), ("cbs", gp)], writes=[("M", hh)])
            for hl in range(8):
                h = 8 * hh + hl
                P.pe(lambda e, hl=hl, h=h: e.matmul(Fy[hh][:, hl * 64:(hl + 1) * 64], M[hh][:, hl, :],
                                                    xdt[:, h * 64:(h + 1) * 64], start=True, stop=True),
                     reads=[("M", hh), ("xdt", gp)], writes=[("F", 6 if hh == 0 else 3)])

        for hh in range(2):
            half_fn(hh)
        for gq in range(2):
            P.pe(lambda e, gq=gq: e.matmul(F[gq][:], xcs[:, 10 + gq, cb], Sb[:, gq * 512:(gq + 1) * 512], start=True, stop=True),
                 reads=[("xcs", 10 + gq), "Sb"], writes=[("F", gq)])
        for gq in range(2):
            ysl = y[:, gq * 512:(gq + 1) * 512]
            P.dve(lambda e, gq=gq, ysl=ysl: e.tensor_tensor(
                out=ysl.rearrange("p (h d) -> p h d", h=8), in0=F[gq][:].rearrange("p (h d) -> p h d", h=8),
                in1=ea[:, 8 * gq:8 * gq + 8].unsqueeze(2).broadcast_to([128, 8, 64]), op=ALU.mult),
                reads=[("F", gq), "ea"], writes=[("y", gp, gq)])
            P.dve(lambda e, gq=gq, ysl=ysl: e.tensor_tensor(out=ysl, in0=Fy[gq][:], in1=ysl, op=ALU.add),
                  reads=[("F", 6 if gq == 0 else 3), ("y", gp, gq)], writes=[("y", gp, gq)])
        P.pool(lambda e: e.tensor_tensor(out=tmpf[:].rearrange("p (h d) -> p h d", h=16), in0=xs3,
                                         in1=dsk[:].unsqueeze(2).broadcast_to([128, 16, 64]), op=ALU.mult),
               reads=[("xs_tok", gp), "dsk"], writes=["tmpf"])
        P.pool(lambda e: e.tensor_tensor(out=y[:], in0=y[:], in1=tmpf[:], op=ALU.add),
               reads=[("y", gp, 0), ("y", gp, 1), "tmpf"], writes=[("y", gp, 0), ("y", gp, 1)])
        P.pool(lambda e: e.tensor_tensor(out=y[:], in0=y[:], in1=zs[:], op=ALU.mult),
               reads=[("y", gp, 0), ("y", gp, 1), ("zs", gp, 0), ("zs", gp, 1)], writes=[("y", gp, 0), ("y", gp, 1)])
        for gq in range(2):
            P.act(lambda e, gq=gq: e.activation(out=tmpf[:, gq * 512:(gq + 1) * 512], in_=y[:, gq * 512:(gq + 1) * 512],
                                                func=AF.Square, accum_out=gss[:, gq:gq + 1]),
                  reads=[("y", gp, gq)], writes=["gss", "tmpf"])
        P.act(lambda e: e.activation(out=gss[:], in_=gss[:], func=AF.Ln, scale=1.0 / 512, bias=EPS),
              reads=["gss"], writes=["gss"])
        P.act(lambda e: e.activation(out=gss[:], in_=gss[:], func=AF.Exp, scale=-0.5), reads=["gss"], writes=["gss"])
        for gq in range(2):
            P.dve(lambda e, gq=gq: e.scalar_tensor_tensor(
                out=ysb[:, gq * 512:(gq + 1) * 512], in0=y[:, gq * 512:(gq + 1) * 512], scalar=gss[:, gq:gq + 1],
                in1=ssdw[:, gq * 512:(gq + 1) * 512], op0=ALU.mult, op1=ALU.mult),
                reads=[("y", gp, gq), "gss", "ssdw"], writes=[("ysb", gp, gq)])
        for ch in range(8):
            P.pe(lambda e, ch=ch: e.transpose(pT[:, ch * 128:(ch + 1) * 128], ysb[:, ch * 128:(ch + 1) * 128], ident[:]),
                 reads=[("ysb", gp, ch // 4), "ident"], writes=["pT"])
        P.act(lambda e: e.copy(yTt[s][:, :, cb], pT[:].rearrange("p (c t) -> p c t", c=8)), reads=["pT"],
              writes=[("yTt", s, b)])
        for gq in range(2):
            P.pe(lambda e, gq=gq: e.matmul(F[4 + gq][:], B_tok[:, gq * 128:(gq + 1) * 128], xw[:, gq * 512:(gq + 1) * 512],
                                           start=True, stop=True),
                 reads=[("B_tok", gp), ("xw", gp)], writes=[("F", 4 + gq)])
        P.pool(lambda e: e.tensor_tensor(out=S[:].rearrange("p (h d) -> p h d", h=16),
                                         in0=S[:].rearrange("p (h d) -> p h d", h=16),
                                         in1=etot[:].unsqueeze(2).broadcast_to([128, 16, 64]), op=ALU.mult),
               reads=["S", "etot"], writes=["S"])
        for gq in range(2):
            P.dve(lambda e, gq=gq: e.tensor_tensor(out=S[:, gq * 512:(gq + 1) * 512], in0=F[4 + gq][:],
                                                   in1=S[:, gq * 512:(gq + 1) * 512], op=ALU.add),
                  reads=[("F", 4 + gq), "S"], writes=["S"])
        P.act(lambda e: e.copy(Sb[:], S[:]), reads=["S"], writes=["Sb"])

    def store(t):
        s = t % 2
        P.dma("sp", yT.ap()[0:1024, t * TT:(t + 1) * TT].rearrange("(c p) t -> p c t", p=128), yTt[s][:],
              reads=[("yTt", s, b) for b in range(4)], writes=[("yT", t)])

    for t in range(NT):
        norm(t)
        dt_tile(t)
        proj_conv(t)
        for b in range(4):
            block(t, b)
        store(t)
    P.emit()
    C.close()
    return P.stats


IN_SHAPES["ev_conv_wT"] = [128, 12, 4]
IN_SHAPES["ev_conv_bT"] = [128, 12]


def phase_wout(nc, dram, src, yT, dst):
    C = Ctx(nc)
    P = Prog(nc)
    TT = 512
    NT = T // TT
    wo = dram["ev_w_out"].ap()[0]
    Wo = C.sb([128, 16, DM], BF16)
    for i in range(4):
        P.dma("pool", Wo[:, 4 * i:4 * i + 4, :], wo.rearrange("(c p) n -> p c n", p=128)[:, 4 * i:4 * i + 4, :],
              writes=[("Wo", i)])
    at = [C.sb([128, 16, TT], BF16) for _ in range(2)]
    xt = [C.sb([128, 4, DM], F32) for _ in range(2)]
    pY = [C.ps([128, 512], F32) for _ in range(4)]
    srca, dsta = src.ap(), dst.ap()

    def load(t):
        s = t % 2
        P.dma("sp", xt[s][:], srca[t * TT:(t + 1) * TT, :].rearrange("(b p) d -> p b d", p=128),
              writes=[("xt", s, b) for b in range(4)])
        P.dma("sp", at[s][:], yT.ap()[:, t * TT:(t + 1) * TT].rearrange("(c p) t -> p c t", p=128), writes=[("at", s)])

    def comp(t):
        s = t % 2
        for b in range(4):
            for half in range(2):
                k = (b * 2 + half) % 4
                for c in range(16):
                    P.pe(lambda e, c=c, b=b, half=half, k=k: e.matmul(
                        pY[k][:], at[s][:, c, b * 128:(b + 1) * 128], Wo[:, c, half * 512:(half + 1) * 512],
                        start=(c == 0), stop=(c == 15)),
                        reads=[("at", s), ("Wo", c // 4)], writes=[("pY", k)])
                xs = xt[s][:, b, half * 512:(half + 1) * 512]
                P.dve(lambda e, xs=xs, k=k: e.tensor_tensor(out=xs, in0=pY[k][:], in1=xs, op=ALU.add),
                      reads=[("pY", k), ("xt", s, b)], writes=[("xt", s, b)])
        P.dma("sp", dsta[t * TT:(t + 1) * TT, :].rearrange("(b p) d -> p b d", p=128), xt[s][:],
              reads=[("xt", s, b) for b in range(4)], writes=[("dst", t)])

    load(0)
    load(1)
    for t in range(NT):
        comp(t)
        if t + 2 < NT:
            load(t + 2)
    P.emit()
    C.close()
    return P.stats


def all_consts():
    c = host_consts_all()
    c.update(consts_swa())
    c.update(consts_ssd())
    return c


_CACHE = {}


def build_full():
    nc = bass.Bass("TRN2", target_bir_lowering=False)
    consts = all_consts()
    dram = declare(nc, consts)
    out = nc.dram_tensor("out", [T, DM], F32, kind="ExternalOutput")
    S = scratch(nc)
    yT = nc.dram_tensor("yT", [2048, T], BF16, kind="Internal")
    stats = {}
    with nc.named_scope("ssd"):
        stats["ssd"] = phase_ssd(nc, dram, dram["x"], yT)
    with nc.named_scope("swa"):
        stats["swa"] = phase_swa(nc, dram, dram["x"], yT)
    with nc.named_scope("wout"):
        stats["wout"] = phase_wout(nc, dram, dram["x"], yT, S["r1"])
    with nc.named_scope("mlp0"):
        stats["mlp0"] = phase_mlp(nc, dram, S["r1"], S["r2"], 0)
    with nc.named_scope("qkv1"):
        stats["qkv1"] = phase_qkv1(nc, dram, S["r2"], S["qTs"], S["kTs"], S["vs"])
    with nc.named_scope("attn1"):
        stats["attn1"] = phase_diffattn(nc, dram, S["qTs"], S["kTs"], S["vs"], S["aoT"])
    with nc.named_scope("mlp1"):
        stats["mlp1"] = phase_mlp(nc, dram, S["r2"], out, 1, pre=(S["aoT"], dram["od_w_out"].ap()[0]))
    return nc, consts, stats


def kernel(**inputs):
    inp = {k: np.asarray(v) for k, v in inputs.items()}
    if "nc" not in _CACHE:
        _CACHE["nc"], _CACHE["consts"], _CACHE["stats"] = build_full()
    nc, consts = _CACHE["nc"], _CACHE["consts"]
    lay = host_layout_inputs(inp)
    shared = {}
    for k in IN_SHAPES:
        if k == "x":
            continue
        if k in lay:
            shared[k] = lay[k].astype(np.float32)
        else:
            shared[k] = np.ascontiguousarray(inp[k], dtype=np.float32)
    shared.update(consts)
    in_maps = []
    for i in range(NCORES):
        d = dict(shared)
        d["x"] = np.ascontiguousarray(inp["x"][i], dtype=np.float32)
        in_maps.append(d)
    res = run_bass_kernel_spmd(nc, in_maps, core_ids=list(range(NCORES)))
    return np.stack([np.asarray(res.results[i]["out"], dtype=np.float32) for i in range(NCORES)], axis=0)
```
